# Optimizing a Trainium2 kernel written in Bass

```python
import math
import jax, jax.numpy as jnp
from jax import lax
import numpy as np

D_MODEL = 1024
BATCH = 32
SEQ = 2048
DEPTH = 1

RNN_WIDTH = D_MODEL * 5 // 4
RNN_BLOCKS = 10
RNN_BLOCK = RNN_WIDTH // RNN_BLOCKS
RNN_CONV = 4
LRU_C = 8.0
HEAD_DIM = 128
KV_HEADS = 4
DILATED_CONFIGS = ((128, 1), (512, 4), (2048, 16))
N_GROUPS = len(DILATED_CONFIGS)
Q_HEADS = N_GROUPS * KV_HEADS
ATTN_BLOCK = 128
REL_BUCKETS = 32
REL_MAX_DIST = 2048
FFN_WIDTH = 3 * D_MODEL
FFN_CONV = 3
EPS = 1e-6

IN_SPLITS = (RNN_WIDTH, Q_HEADS * HEAD_DIM, KV_HEADS * HEAD_DIM, KV_HEADS * HEAD_DIM, D_MODEL, D_MODEL)
IN_WIDTH = sum(IN_SPLITS)

kernel_name = "hybrid_rglru_dilated_attn_convffn"


def rms_norm(x, g):
    xf = x.astype(jnp.float32)
    y = xf * lax.rsqrt(jnp.mean(xf * xf, axis=-1, keepdims=True) + EPS)
    return (y * g.astype(jnp.float32)).astype(x.dtype)


def causal_dwconv(x, w, b):
    k = w.shape[0]
    y = lax.conv_general_dilated(x, w[:, None, :].astype(x.dtype), window_strides=(1,),
                                 padding=[(k - 1, 0)], dimension_numbers=("NWC", "WIO", "NWC"),
                                 feature_group_count=x.shape[-1])
    return y + b.astype(x.dtype)


def _t5_bucket(dist):
    max_exact = REL_BUCKETS // 2
    d = np.maximum(dist, 1).astype(np.float32)
    large = max_exact + np.log(d / max_exact) / math.log(REL_MAX_DIST / max_exact) * (REL_BUCKETS - max_exact)
    large = np.minimum(large.astype(np.int32), REL_BUCKETS - 1)
    return np.where(dist < max_exact, dist, large).astype(np.int32)


def _band_structure(n_blocks, dilation, n_back):
    qi = np.arange(ATTN_BLOCK)[None, :, None]
    kj = np.arange(2 * ATTN_BLOCK)[None, None, :]
    nb = np.arange(n_blocks)[:, None, None]
    delta = ATTN_BLOCK + qi - kj
    mask = (delta >= 0) & (delta <= n_back) & ((nb - 1) * ATTN_BLOCK + kj >= 0)
    bucket = _t5_bucket(np.maximum(delta[0], 0) * dilation)
    return mask, bucket


def dilated_group(q, k, v, bias_g, window, dilation):
    B, H, S, hd = q.shape
    r = dilation
    n_back = window // dilation
    M = S // r
    nb = -(-M // ATTN_BLOCK)
    Mp = nb * ATTN_BLOCK

    def to_sub(t):
        return t.reshape(B, H, M, r, hd).transpose(0, 1, 3, 2, 4)

    qs = jnp.pad(to_sub(q), ((0, 0), (0, 0), (0, 0), (0, Mp - M), (0, 0)))
    qs = qs.reshape(B, H, r, nb, ATTN_BLOCK, hd)

    def key_blocks(t):
        ts = jnp.pad(to_sub(t), ((0, 0), (0, 0), (0, 0), (ATTN_BLOCK, Mp - M), (0, 0)))
        ts = ts.reshape(B, H, r, nb + 1, ATTN_BLOCK, hd)
        return jnp.concatenate([ts[:, :, :, :-1], ts[:, :, :, 1:]], axis=4)

    kb = key_blocks(k)
    vb = key_blocks(v)
    mask, bucket = _band_structure(nb, r, n_back)
    bias = bias_g[jnp.asarray(bucket)].astype(jnp.float32).transpose(2, 0, 1)

    logits = jnp.einsum("bhrnqd,bhrnkd->bhrnqk", qs, kb).astype(jnp.float32) * (HEAD_DIM ** -0.5)
    logits = logits + bias[None, :, None, None]
    logits = jnp.where(jnp.asarray(mask)[None, None, None], logits, -jnp.inf)
    mx = jnp.max(logits, axis=-1, keepdims=True)
    p = jnp.exp(logits - mx)
    den = jnp.sum(p, axis=-1, keepdims=True)
    o = jnp.einsum("bhrnqk,bhrnkd->bhrnqd", p, vb.astype(jnp.float32)) / den
    lse = (mx + jnp.log(den))[..., 0]

    o = o.reshape(B, H, r, Mp, hd)[:, :, :, :M].transpose(0, 1, 3, 2, 4).reshape(B, H, S, hd)
    lse = lse.reshape(B, H, r, Mp)[:, :, :, :M].transpose(0, 1, 3, 2).reshape(B, H, S)
    return o, lse


def rg_lru(xc, w_a, b_a, w_x, b_x, lam):
    B, S, C = xc.shape
    xb = xc.reshape(B, S, RNN_BLOCKS, RNN_BLOCK)
    r = jax.nn.sigmoid(jnp.einsum("bsnc,ncd->bsnd", xb, w_a).reshape(B, S, C).astype(jnp.float32)
                       + b_a.astype(jnp.float32))
    i = jax.nn.sigmoid(jnp.einsum("bsnc,ncd->bsnd", xb, w_x).reshape(B, S, C).astype(jnp.float32)
                       + b_x.astype(jnp.float32))
    log_a = -LRU_C * r * jax.nn.softplus(-lam.astype(jnp.float32))
    a = jnp.exp(log_a)
    u = jnp.sqrt(-jnp.expm1(2.0 * log_a)) * (i * xc.astype(jnp.float32))

    def step(h, inp):
        a_t, u_t = inp
        h = a_t * h + u_t
        return h, h

    _, hs = lax.scan(step, jnp.zeros((B, C), jnp.float32), (a.transpose(1, 0, 2), u.transpose(1, 0, 2)))
    return hs.transpose(1, 0, 2)


def setup_inputs(seed: int = 0) -> dict:
    key = jax.random.key(seed)
    ks = jax.random.split(key, 24)
    f32 = jnp.float32
    L, D = DEPTH, D_MODEL

    def nrm(k, shape, scale):
        return jax.random.normal(k, shape, f32) * scale

    u = jax.random.uniform(ks[10], (L, RNN_WIDTH), f32, minval=0.9, maxval=0.999)
    p = u ** (1.0 / LRU_C)
    lru_lambda = jnp.log(p) - jnp.log1p(-p)
    return {
        "x": nrm(ks[0], (BATCH, SEQ, D), 1.0),
        "rel_bias": nrm(ks[1], (REL_BUCKETS, Q_HEADS), 0.1),
        "norm_mix_pre": 1.0 + nrm(ks[2], (L, D), 0.05),
        "norm_mix_post": 1.0 + nrm(ks[3], (L, D), 0.05),
        "w_in": nrm(ks[4], (L, D, IN_WIDTH), D ** -0.5),
        "conv_rnn_w": nrm(ks[5], (L, RNN_CONV, RNN_WIDTH), RNN_CONV ** -0.5),
        "conv_rnn_b": nrm(ks[6], (L, RNN_WIDTH), 0.01),
        "w_rg_a": nrm(ks[7], (L, RNN_BLOCKS, RNN_BLOCK, RNN_BLOCK), RNN_BLOCK ** -0.5),
        "b_rg_a": nrm(ks[8], (L, RNN_WIDTH), 0.01),
        "w_rg_x": nrm(ks[9], (L, RNN_BLOCKS, RNN_BLOCK, RNN_BLOCK), RNN_BLOCK ** -0.5),
        "b_rg_x": nrm(ks[11], (L, RNN_WIDTH), 0.01),
        "lru_lambda": lru_lambda,
        "w_branch_rnn": nrm(ks[12], (L, RNN_WIDTH, D), RNN_WIDTH ** -0.5),
        "w_branch_att": nrm(ks[13], (L, KV_HEADS * HEAD_DIM, D), (KV_HEADS * HEAD_DIM) ** -0.5),
        "w_out": nrm(ks[14], (L, D, D), D ** -0.5),
        "norm_ffn_pre": 1.0 + nrm(ks[15], (L, D), 0.05),
        "norm_ffn_post": 1.0 + nrm(ks[16], (L, D), 0.05),
        "w_ffn_gate": nrm(ks[17], (L, D, FFN_WIDTH), D ** -0.5),
        "w_ffn_up": nrm(ks[18], (L, D, FFN_WIDTH), D ** -0.5),
        "conv_ffn_w": nrm(ks[19], (L, FFN_CONV, FFN_WIDTH), FFN_CONV ** -0.5),
        "conv_ffn_b": nrm(ks[20], (L, FFN_WIDTH), 0.01),
        "w_ffn_down": nrm(ks[21], (L, FFN_WIDTH, D), FFN_WIDTH ** -0.5),
    }


def reference(x, rel_bias, norm_mix_pre, norm_mix_post, w_in, conv_rnn_w, conv_rnn_b, w_rg_a, b_rg_a,
              w_rg_x, b_rg_x, lru_lambda, w_branch_rnn, w_branch_att, w_out, norm_ffn_pre, norm_ffn_post,
              w_ffn_gate, w_ffn_up, conv_ffn_w, conv_ffn_b, w_ffn_down):
    B, S, D = x.shape
    split_idx = [int(s) for s in np.cumsum(IN_SPLITS)[:-1]]
    h = x
    for l in range(DEPTH):
        hn = rms_norm(h, norm_mix_pre[l])
        proj = hn @ w_in[l]
        xr, q, k, v, g_rnn, g_att = jnp.split(proj, split_idx, axis=-1)

        xc = causal_dwconv(xr, conv_rnn_w[l], conv_rnn_b[l])
        y_rnn = rg_lru(xc, w_rg_a[l], b_rg_a[l], w_rg_x[l], b_rg_x[l], lru_lambda[l]).astype(x.dtype)

        qg = q.reshape(B, S, N_GROUPS, KV_HEADS, HEAD_DIM).transpose(2, 0, 3, 1, 4)
        kh = k.reshape(B, S, KV_HEADS, HEAD_DIM).transpose(0, 2, 1, 3)
        vh = v.reshape(B, S, KV_HEADS, HEAD_DIM).transpose(0, 2, 1, 3)
        outs, lses = [], []
        for g, (window, dilation) in enumerate(DILATED_CONFIGS):
            o_g, lse_g = dilated_group(qg[g], kh, vh, rel_bias[:, g * KV_HEADS:(g + 1) * KV_HEADS],
                                       window, dilation)
            outs.append(o_g)
            lses.append(lse_g)
        alpha = jax.nn.softmax(jnp.stack(lses, axis=0), axis=0)
        o_att = jnp.sum(alpha[..., None] * jnp.stack(outs, axis=0), axis=0)
        o_att = o_att.transpose(0, 2, 1, 3).reshape(B, S, KV_HEADS * HEAD_DIM).astype(x.dtype)

        merged = (jax.nn.sigmoid(g_rnn) * (y_rnn @ w_branch_rnn[l])
                  + jax.nn.sigmoid(g_att) * (o_att @ w_branch_att[l]))
        mix = merged @ w_out[l]
        h = h + rms_norm(mix, norm_mix_post[l])

        hn = rms_norm(h, norm_ffn_pre[l])
        gate = causal_dwconv(hn @ w_ffn_gate[l], conv_ffn_w[l], conv_ffn_b[l])
        ff = (jax.nn.gelu(gate, approximate=True) * (hn @ w_ffn_up[l])) @ w_ffn_down[l]
        h = h + rms_norm(ff, norm_ffn_post[l])
    return h
```

```python
import contextlib
import math
import numpy as np
import concourse.bass as bass
import concourse.mybir as mybir
from concourse.bass_utils import run_bass_kernel_spmd

F32 = mybir.dt.float32
BF16 = mybir.dt.bfloat16
AF = mybir.ActivationFunctionType
ALU = mybir.AluOpType

ENGS = ["pe", "act", "dve", "pool", "sp"]
EPOCH_LIMIT = 12000


class Res:
    __slots__ = ("name", "w", "r", "const")

    def __init__(self, name, after=(), const=False):
        self.name = name
        self.w = None
        self.r = {}
        self.const = const
        for o in after:
            if o.w is not None:
                self._addr(o.w)
            for t in o.r.values():
                self._addr(t)

    def _addr(self, t):
        k = t[1]
        o = self.r.get(k)
        if o is None or o[2] < t[2]:
            self.r[k] = t


class Sched:
    def __init__(self):
        self.ops = {e: [] for e in ENGS}
        self.count = {e: 0 for e in ENGS}
        self.epoch = {e: 0 for e in ENGS}
        self.known = {e: {} for e in ENGS}
        self.prev_epochs = {e: {} for e in ENGS}
        self.dma_cnt = {}
        self.pend_r = {e: [] for e in ENGS}
        self.pend_w = {e: [] for e in ENGS}
        self.semkeys = set()

    def _need(self, eng, tok, waits):
        key, val, snap = tok[1], tok[2], tok[3]
        kn = self.known[eng]
        if kn.get(key, 0) >= val:
            return
        waits.append((key, val))
        kn[key] = val
        for k, v in snap.items():
            if kn.get(k, 0) < v:
                kn[k] = v

    def op(self, eng, fn, reads=(), writes=(), dma=None, signal=True):
        waits = []
        isdma = dma is not None
        for r in reads:
            t = r.w
            if t is not None:
                if t[0] == eng and not isdma and eng == "pe":
                    continue
                self._need(eng, t, waits)
        for w in writes:
            t = w.w
            if t is not None and (t[0] != eng or isdma):
                self._need(eng, t, waits)
            for t in w.r.values():
                if t[0] != eng or isdma:
                    self._need(eng, t, waits)
        if isdma:
            key = ("dma", dma)
            ndma = len(fn) if isinstance(fn, list) else 1
            self.dma_cnt[key] = self.dma_cnt.get(key, 0) + 16 * ndma
            val = self.dma_cnt[key]
            tok = ("dma", key, val, dict(self.known[eng]))
            inc = (key, 16)
            self.semkeys.add(key)
        elif signal:
            if self.count[eng] >= EPOCH_LIMIT:
                self.prev_epochs[eng][(eng, self.epoch[eng])] = self.count[eng]
                self.epoch[eng] += 1
                self.count[eng] = 0
            self.count[eng] += 1
            key = (eng, self.epoch[eng])
            val = self.count[eng]
            snap = dict(self.known[eng])
            snap.update(self.prev_epochs[eng])
            tok = (eng, key, val, snap)
            inc = (key, 1)
            self.semkeys.add(key)
        else:
            tok = None
            inc = None
        if tok is None:
            self.pend_r[eng].extend(r for r in reads if not r.const)
            self.pend_w[eng].extend(writes)
        else:
            if not isdma:
                for r in self.pend_r[eng]:
                    r._addr(tok)
                for w in self.pend_w[eng]:
                    w.w = tok
                    w.r = {}
                self.pend_r[eng] = []
                self.pend_w[eng] = []
            for r in reads:
                if not r.const:
                    r._addr(tok)
            for w in writes:
                w.w = tok
                w.r = {}
        self.ops[eng].append((waits, fn, inc))
        return tok

    def wait_tokens(self, eng, toks):
        waits = []
        for t in toks:
            if t is not None:
                self._need(eng, t, waits)
        if waits:
            self.ops[eng].append((waits, None, None))


def _replay(sched, eng, e, sems):
    for waits, fn, inc in sched.ops[eng]:
        for key, val in waits:
            e.wait_ge(sems[key], val)
        if fn is None:
            continue
        if isinstance(fn, list):
            for meth, kw in fn:
                getattr(e, meth)(**kw).then_inc(sems[inc[0]], 16)
            continue
        meth, kw = fn
        ins = getattr(e, meth)(**kw)
        if inc is not None:
            ins.then_inc(sems[inc[0]], inc[1])


def run_block(nc, sched):
    with contextlib.ExitStack() as st:
        sems = {}
        for i, key in enumerate(sorted(sched.semkeys, key=str)):
            sems[key] = st.enter_context(nc.semaphore("s%d" % i))
        block = st.enter_context(nc.Block())

        @block.tensor
        def _(e):
            _replay(sched, "pe", e, sems)

        @block.scalar
        def _(e):
            _replay(sched, "act", e, sems)

        @block.vector
        def _(e):
            _replay(sched, "dve", e, sems)

        @block.gpsimd
        def _(e):
            _replay(sched, "pool", e, sems)

        @block.sync
        def _(e):
            _replay(sched, "sp", e, sems)


D = 1024
SEQ = 2048
NCORES = 8
RNN_W = 1280
NRC = 10
HD = 128
NH = 4
DIL = ((128, 1), (512, 4), (2048, 16))
FFN = 3072
NFC = 24
EPS = 1e-6
IN_W = 5888
OFF_Q = 1280
OFF_K = OFF_Q + 1536
OFF_V = OFF_K + 512
OFF_GR = OFF_V + 512
OFF_GA = OFF_GR + 1024
NEG = -30000.0
NV = 192
SM_SCALE = HD ** -0.5
SCHED_STATS = {}


def _t5_bucket(dist):
    max_exact = 16
    d = np.maximum(dist, 1).astype(np.float32)
    large = max_exact + np.log(d / max_exact) / math.log(2048 / max_exact) * (32 - max_exact)
    large = np.minimum(large.astype(np.int32), 31)
    return np.where(dist < max_exact, dist, large).astype(np.int32)


def bcast_free(ap2d, n):
    return bass.AP(ap2d.tensor, ap2d.offset, [list(ap2d.ap[0]), list(ap2d.ap[1]), [0, n]])


def build(nseq, debug=False, stop_after=None):
    nc = bass.Bass("TRN2", target_bir_lowering=False)
    T = nseq * SEQ

    def dram(name, shape, kind="ExternalInput"):
        return nc.dram_tensor(name, list(shape), F32, kind=kind).ap()

    x_d = dram("x", [T, D])
    out_d = dram("out", [T, D], "ExternalOutput")
    w_in_d = dram("w_in", [D, IN_W])
    w_rga_d = dram("w_rg_a", [NRC * 128, 128])
    w_rgx_d = dram("w_rg_x", [NRC * 128, 128])
    w_br_d = dram("w_branch_rnn", [RNN_W, D])
    w_ba_d = dram("w_branch_att", [512, D])
    w_out_d = dram("w_out", [D, D])
    w_g_d = dram("w_ffn_gate", [D, FFN])
    w_u_d = dram("w_ffn_up", [D, FFN])
    w_d_d = dram("w_ffn_down", [FFN, D])
    relb_d = dram("rel_bias", [32, 12])
    oh_d = dram("oh", [32, 3 * 129])
    vecs_d = dram("vecs", [128, NV])
    gpost_d = dram("gpost", [128, D])
    gpost2_d = dram("gpost2", [128, D])
    ident_d = dram("ident", [128, 128])
    jmat_d = dram("jmat", [128, 128])
    sc_d = nc.dram_tensor("scratch_ext", [12, 383], F32).ap()
    dbg = {}
    if debug:
        for nm, shp in [("d_hnT", [128, 8 * SEQ]), ("d_yrnn", [128, NRC * SEQ]), ("d_oatt", [128, 4 * SEQ]),
                        ("d_merged", [128, 8 * SEQ]), ("d_E", [128, 12 * 256]), ("d_cfac", [128, 16])]:
            dbg[nm] = dram(nm, shp, "ExternalOutput")

    w_in_v = w_in_d.rearrange("(kc p) n -> p kc n", p=128)
    w_g_v = w_g_d.rearrange("(kc p) n -> p kc n", p=128)
    w_u_v = w_u_d.rearrange("(kc p) n -> p kc n", p=128)
    w_br_v = w_br_d.rearrange("(kc p) n -> p kc n", p=128)
    w_ba_v = w_ba_d.rearrange("(kc p) n -> p kc n", p=128)
    w_out_v = w_out_d.rearrange("(kc p) n -> p kc n", p=128)
    w_d_v = w_d_d.rearrange("(kc p) n -> p kc n", p=128)
    w_rga_v = w_rga_d.rearrange("(c p) n -> p c n", p=128)
    w_rgx_v = w_rgx_d.rearrange("(c p) n -> p c n", p=128)

    BASE = 24576
    LIMIT = 229376
    cur = [BASE]

    def take(nb):
        o = cur[0]
        cur[0] += (nb + 63) // 64 * 64
        assert cur[0] <= LIMIT, "SBUF overflow %d" % cur[0]
        return o

    cnt = [0]

    def sbt(shape, dt, off):
        cnt[0] += 1
        return nc.alloc_sbuf_tensor_at("t%d" % cnt[0], list(shape), dt, offset=off)

    def nbytes(shape, dt):
        return int(np.prod(shape[1:])) * (4 if dt == F32 else 2)

    def new(shape, dt):
        return sbt(shape, dt, take(nbytes(shape, dt)))

    ident_bf = new([128, 128], BF16)
    ones_bf = new([128, 128], BF16)
    vecs = new([128, NV], F32)
    cf = new([128, 32], F32)
    Et = new([128, 12, 2, 128], BF16)
    gpost = new([128, D], F32)
    gpost2 = new([128, D], F32)
    w_out_bf = new([128, 8, D], BF16)
    stats = new([128, 64], F32)
    carry = new([128, NFC, 2], F32)
    stg = [new([128, 2048], F32) for _ in range(2)]
    wbf = [new([128, 4096], BF16) for _ in range(2)]
    R1_off = take(8 * SEQ * 2)
    R2_off = take(14 * SEQ * 2)
    R3_off = take(8 * SEQ * 2)
    W_off = cur[0]
    W_size = LIMIT - W_off
    R3_size = 8 * SEQ * 2

    hnT = sbt([128, 8, SEQ], BF16, R1_off)
    yrnn = sbt([128, NRC, SEQ], BF16, R2_off)
    oatt = sbt([128, 4, SEQ], BF16, R2_off + NRC * SEQ * 2)
    wdn = sbt([128, NFC, D], BF16, R2_off)
    R2_spare = R2_off + NFC * D * 2
    merged = sbt([128, 8, SEQ], BF16, R3_off)
    ffT = sbt([128, NFC, 512], BF16, R3_off)
    R3_spare = R3_off + NFC * 512 * 2

    class Region:
        def __init__(self, spans):
            self.spans = [list(sp) for sp in spans]
            self.i = 0
            self.o = self.spans[0][0]

        def new(self, shape, dt):
            nb = (nbytes(shape, dt) + 63) // 64 * 64
            while True:
                sp = self.spans[self.i]
                if self.o + nb <= sp[0] + sp[1]:
                    o = self.o
                    self.o += nb
                    return sbt(shape, dt, o)
                self.i += 1
                assert self.i < len(self.spans), "region overflow"
                self.o = self.spans[self.i][0]

    WR = Region([(W_off, W_size), (R3_off, R3_size)])
    tmpc = WR.new([128, 64], F32)
    rb = WR.new([32, 12], F32)
    oh = WR.new([32, 3 * 129], F32)
    ext = WR.new([12, 3, 383], F32)
    jm = WR.new([128, 128], F32)
    Hb = [WR.new([128, 256], F32) for _ in range(2)]
    dE = WR.new([128, 12 * 256], F32) if debug else None

    RA = Region([(R3_off, R3_size)])
    xt = [RA.new([128, D], F32) for _ in range(2)]
    hnb = [RA.new([128, D], BF16) for _ in range(2)]
    junkA = RA.new([128, D], BF16)

    RB = Region([(R3_off, R3_size), (R2_off, NRC * SEQ * 2), (W_off, W_size)])
    qT = [RB.new([128, SEQ], BF16) for _ in range(3)]
    kT = RB.new([128, SEQ], BF16)
    vT = RB.new([128, SEQ], BF16)
    Vb = [RB.new([128, 16, 128], BF16) for _ in range(3)]
    acc = RB.new([128, 2, SEQ], F32)
    Pf = [RB.new([128, 256], F32) for _ in range(2)]
    Pb = [RB.new([128, 256], BF16) for _ in range(2)]
    rcp = RB.new([128, SEQ], F32)

    RC = Region([(R3_off, R3_size), (W_off, W_size)])
    xr = RC.new([128, SEQ + 4], F32)
    xc = RC.new([128, SEQ], F32)
    xcb = RC.new([128, SEQ], BF16)
    rr = RC.new([128, SEQ], F32)
    ii = RC.new([128, SEQ], F32)

    dbuf = sbt([128, SEQ], F32, R2_off)
    RD = Region([(W_off, W_size)])
    tm = [RD.new([128, 512], F32) for _ in range(2)]

    RD2 = Region([(W_off, W_size), (R2_spare, 8192)])
    xt2 = RD2.new([128, D], F32)
    ht = [RD2.new([128, D], F32) for _ in range(2)]
    hn2b = RD2.new([128, D], BF16)
    junk2 = RD2.new([128, D], BF16)

    RE = Region([(W_off, W_size), (R3_spare, 8192), (R2_spare, 8192)])
    gt = [RE.new([128, 516], F32) for _ in range(2)]
    gc = [RE.new([128, 512], F32) for _ in range(2)]
    ge = [RE.new([128, 512], F32) for _ in range(2)]
    ot = RE.new([128, D], F32)
    hbuf = RE.new([128, D], F32)
    junk3 = RE.new([128, D], BF16)

    psF = nc.alloc_psum_tensor("psF", [128, 6, 512], F32)
    psT = nc.alloc_psum_tensor("psT", [128, 2, 1024], BF16)
    bankF = [Res("bankF%d" % i) for i in range(6)]
    bankT = [Res("bankT%d" % i) for i in range(2)]
    rot = {"F": 0, "T": 0, "P": 0}

    def nextF():
        b = rot["F"]
        rot["F"] = (b + 1) % 6
        return b

    def nextT():
        b = rot["T"]
        rot["T"] = (b + 1) % 2
        return b

    def nextP():
        b = rot["P"]
        rot["P"] = (b + 1) % 3
        return b

    S = Sched()
    dbg_toks = []

    def OP(eng, meth, reads=(), writes=(), signal=True, **kw):
        return S.op(eng, (meth, kw), reads=reads, writes=writes, signal=signal)

    def DMA(out, in_, reads=(), writes=(), key=None):
        return S.op("sp", [("dma_start", dict(out=out, in_=in_))], reads=reads, writes=writes, dma=key)

    def ACTV(out, in_, func, reads, writes, **kw):
        return OP("act", "activation", reads, writes, out=out, in_=in_, func=func, **kw)

    CR = lambda n: Res(n, const=True)
    r_ident, r_ones, r_vecs, r_cf, r_E = CR("ident"), CR("ones"), CR("vecs"), CR("cf"), CR("E")
    r_gpost, r_gpost2, r_wout = CR("gpost"), CR("gpost2"), CR("wout")
    r_stg = [Res("stg0"), Res("stg1")]
    r_wbf = [Res("wbf0"), Res("wbf1")]
    stg_i = [0]
    wbf_i = [0]

    def stage(pieces):
        si = stg_i[0]
        stg_i[0] = (si + 1) % 2
        o = 0
        views = []
        calls = []
        for src, shape in pieces:
            n = int(np.prod(shape))
            if len(shape) == 2:
                dst = stg[si][:, o:o + n].rearrange("p (a b) -> p a b", b=shape[1])
            else:
                dst = stg[si][:, o:o + n]
            calls.append(("dma_start", dict(out=dst, in_=src)))
            views.append(dst)
            o += n
        assert o <= 2048
        S.op("sp", calls, writes=[r_stg[si]], dma="stg%d" % si)
        return si, views

    def cast(si, src, dst, dres, scale=None, eng="pool"):
        if scale is None:
            OP(eng, "tensor_copy", [r_stg[si]], [dres], out=dst, in_=src)
        else:
            n = src.shape[-1]
            OP(eng, "tensor_tensor", [r_stg[si], r_vecs], [dres], out=dst, in0=src, in1=bcast_free(scale, n), op=ALU.mult)

    def next_wbf():
        i = wbf_i[0]
        wbf_i[0] = (i + 1) % 2
        return i

    gpre = vecs[:, 176:184]
    gpre2 = vecs[:, 184:192]
    DMA(vecs[:], vecs_d, writes=[r_vecs], key="c_vecs")
    DMA(gpost[:], gpost_d, writes=[r_gpost], key="c_gpost")
    DMA(gpost2[:], gpost2_d, writes=[r_gpost2], key="c_gpost2")
    OP("pool", "memset", [], [r_ones], ap=ones_bf[:], constant=1.0)
    si, (v,) = stage([(ident_d, [128])])
    OP("dve", "tensor_copy", [r_stg[si]], [r_ident], out=ident_bf[:], in_=v)

    r_tmpc = Res("tmpc")
    lam = vecs[:, 70:80]
    xx, dd, zz, z2, pp = (tmpc[:, 0:10], tmpc[:, 10:20], tmpc[:, 20:30], tmpc[:, 30:40], tmpc[:, 40:50])
    TC = [r_tmpc]
    ACTV(xx, lam, AF.Exp, [r_vecs], TC, scale=-1.0)
    OP("dve", "tensor_scalar", TC, TC, out=dd, in0=xx, scalar1=2.0, scalar2=None, op0=ALU.add)
    OP("dve", "reciprocal", TC, TC, out=dd, in_=dd)
    OP("dve", "tensor_tensor", TC, TC, out=zz, in0=xx, in1=dd, op=ALU.mult)
    OP("dve", "tensor_tensor", TC, TC, out=z2, in0=zz, in1=zz, op=ALU.mult)
    OP("dve", "tensor_scalar", TC, TC, out=pp, in0=z2, scalar1=1.0 / 9, scalar2=1.0 / 7, op0=ALU.mult, op1=ALU.add)
    for cst in (1.0 / 5, 1.0 / 3, 1.0):
        OP("dve", "tensor_tensor", TC, TC, out=pp, in0=pp, in1=z2, op=ALU.mult)
        OP("dve", "tensor_scalar", TC, TC, out=pp, in0=pp, scalar1=cst, scalar2=None, op0=ALU.add)
    OP("dve", "tensor_tensor", TC, TC, out=pp, in0=pp, in1=zz, op=ALU.mult)
    OP("dve", "tensor_scalar", TC, [r_cf], out=cf[:, 0:10], in0=pp, scalar1=-16.0, scalar2=None, op0=ALU.mult)
    OP("dve", "tensor_scalar", TC + [r_cf], [r_cf], out=cf[:, 10:20], in0=pp, scalar1=-32.0, scalar2=None, op0=ALU.mult)

    r_rb, r_oh, r_ext, r_jm, r_sc = Res("rb"), Res("oh"), Res("ext"), Res("jm"), Res("sc")
    r_H = [Res("H0"), Res("H1")]
    DMA(rb[:], relb_d, writes=[r_rb], key="c_rb")
    DMA(oh[:], oh_d, writes=[r_oh], key="c_oh")
    DMA(jm[:], jmat_d, writes=[r_jm], key="c_jm")
    OP("pool", "memset", [], [r_ext], ap=ext[:], constant=NEG)
    for g in range(3):
        b = nextF()
        OP("pe", "matmul", [r_rb, r_oh], [bankF[b]], out=psF[0:12, b, 0:129], lhsT=rb[:], rhs=oh[:, g * 129:(g + 1) * 129], start=True, stop=True)
        OP("dve", "tensor_copy", [bankF[b]], [r_ext], out=ext[:, g, 127:256], in_=psF[0:12, b, 0:129])
    for g in range(3):
        DMA(sc_d[4 * g:4 * g + 4, :], ext[4 * g:4 * g + 4, g, :], reads=[r_ext], writes=[r_sc], key="c_sc")
    for gh in range(12):
        hb = gh % 2
        src = bass.AP(sc_d.tensor, gh * 383, [[1, 128], [1, 256]])
        DMA(Hb[hb][:], src, reads=[r_sc], writes=[r_H[hb]], key="c_H%d" % hb)
        b = nextF()
        OP("pe", "matmul", [r_jm, r_H[hb]], [bankF[b]], out=psF[:, b, 0:256], lhsT=jm[:], rhs=Hb[hb][:], start=True, stop=True)
        ACTV(Et[:, gh, :, :], psF[:, b, 0:256].rearrange("p (a b) -> p a b", b=128), AF.Exp, [bankF[b]], [r_E])
    for q4 in range(4):
        si, (v,) = stage([(w_out_v[:, :, q4 * 256:(q4 + 1) * 256], [8, 256])])
        cast(si, v, w_out_bf[:, :, q4 * 256:(q4 + 1) * 256], r_wout)
    if debug:
        r_dE = Res("dE")
        OP("dve", "tensor_copy", [r_E], [r_dE], out=dE[:], in_=Et[:].rearrange("p a b c -> p (a b c)"))
        dbg_toks.append(DMA(dbg["d_E"], dE[:], reads=[r_dE], key="dbg"))
        dbg_toks.append(DMA(dbg["d_cfac"][:, 0:16], cf[:, 0:16], reads=[r_cf], key="dbg"))

    out_toks = []
    base_dead = [r_tmpc, r_rb, r_oh, r_ext, r_jm, r_H[0], r_H[1]] + ([r_dE] if debug else [])
    prev_hn2T = None
    evi = [0]

    def evac_copy(dst, src, reads, writes, scale=None):
        evi[0] += 1
        if evi[0] % 2:
            if scale is not None:
                ACTV(dst, src, AF.Copy, reads, writes, scale=scale)
            else:
                ACTV(dst, src, AF.Copy, reads, writes)
        else:
            if scale is not None:
                OP("dve", "tensor_scalar", reads, writes, out=dst, in0=src, scalar1=scale, scalar2=None, op0=ALU.mult)
            else:
                OP("dve", "tensor_copy", reads, writes, out=dst, in_=src)

    def norm_to_T(src_tile, r_src, hb, r_hb, jk, r_jk, r_st, dstT, r_dst, tt, scol):
        ss = stats[:, scol:scol + 1]
        sq = stats[:, scol + 1:scol + 2]
        rs = stats[:, scol + 2:scol + 3]
        ACTV(jk, src_tile, AF.Square, [r_src], [r_jk, r_st], accum_out=ss)
        ACTV(sq, ss, AF.Sqrt, [r_st], [r_st], scale=1.0 / D, bias=EPS)
        OP("dve", "reciprocal", [r_st], [r_st], out=rs, in_=sq)
        OP("dve", "tensor_scalar", [r_src, r_st], [r_hb], out=hb, in0=src_tile, scalar1=rs, scalar2=None, op0=ALU.mult)
        tb = nextT()
        for kc in range(8):
            OP("pe", "transpose", [r_hb, r_ident], [bankT[tb]], signal=(kc == 7),
               out=psT[:, tb, kc * 128:(kc + 1) * 128], in_=hb[:, kc * 128:(kc + 1) * 128], identity=ident_bf[:])
        ACTV(dstT[:, :, tt * 128:(tt + 1) * 128], psT[:, tb, :].rearrange("p (a b) -> p a b", b=128), AF.Copy, [bankT[tb]], [r_dst])

    def proj_fm(wtile, nk, rhs_t, rhs_res, dst_fn, w_res):
        for tg in range(4):
            b = nextF()
            for kc in range(nk):
                OP("pe", "matmul", [w_res] + rhs_res(tg), [bankF[b]], signal=(kc == nk - 1),
                   out=psF[:, b, :], lhsT=wtile[:, kc, :], rhs=rhs_t[:, kc, tg * 512:(tg + 1) * 512], start=(kc == 0), stop=(kc == nk - 1))
            dst_fn(tg, b)

    for s in range(nseq):
        row0 = s * SEQ
        r_hnT = [Res("hnT%d" % t, after=base_dead + ([prev_hn2T[t]] if prev_hn2T else [])) for t in range(16)]
        r_yrnn = [Res("yrnn%d" % c, after=base_dead) for c in range(NRC)]
        r_oatt = [Res("oatt%d" % h, after=base_dead) for h in range(NH)]
        r_merged = [Res("merged%d" % m, after=base_dead) for m in range(8)]
        hn_res = lambda tg: r_hnT[tg * 4:(tg + 1) * 4]

        r_xt = [Res("xt%d" % i, after=base_dead) for i in range(2)]
        r_hnb = [Res("hnb%d" % i, after=base_dead) for i in range(2)]
        r_junkA = Res("junkA", after=base_dead)
        r_stA = Res("statsA")
        for tt in range(16):
            bi = tt % 2
            DMA(xt[bi][:], x_d[row0 + tt * 128:row0 + (tt + 1) * 128, :], writes=[r_xt[bi]], key="xt%d" % bi)
            norm_to_T(xt[bi][:], r_xt[bi], hnb[bi][:], r_hnb[bi], junkA[:], r_junkA, r_stA, hnT, r_hnT[tt], tt, 4 * bi)
        phaseA_res = r_xt + r_hnb + [r_junkA]
        if stop_after == "A":
            break

        dB = phaseA_res + base_dead
        r_q = [Res("q%d" % g, after=dB) for g in range(3)]
        r_k = Res("k", after=dB)
        r_v = Res("v", after=dB)
        r_Vb = [Res("Vb%d" % g, after=dB) for g in range(3)]
        r_acc = Res("acc", after=dB)
        r_Pf = [Res("Pf%d" % i, after=dB) for i in range(2)]
        r_Pb = [Res("Pb%d" % i, after=dB) for i in range(2)]
        r_rcp = Res("rcp", after=dB)
        pcount = 0
        for h in range(NH):
            for j in range(5):
                c0 = (OFF_Q + (j * 4 + h) * 128) if j < 3 else (OFF_K + h * 128 if j == 3 else OFF_V + h * 128)
                si, (v,) = stage([(w_in_v[:, :, c0:c0 + 128], [8, 128])])
                wi = next_wbf()
                wt = wbf[wi][:, 0:1024].rearrange("p (a b) -> p a b", b=128)
                cast(si, v, wt, r_wbf[wi], scale=gpre)
                if j < 3:
                    dstt, dres, scl = qT[j], r_q[j], SM_SCALE
                elif j == 3:
                    dstt, dres, scl = kT, r_k, None
                else:
                    dstt, dres, scl = vT, r_v, None
                proj_fm(wt, 8, hnT, hn_res,
                        lambda tg, b, dstt=dstt, dres=dres, scl=scl: evac_copy(dstt[:, tg * 512:(tg + 1) * 512], psF[:, b, :], [bankF[b]], [dres], scale=scl),
                        r_wbf[wi])
            for g, (win, r) in enumerate(DIL):
                nb = SEQ // r // 128
                for half in range(2):
                    tb = nextT()
                    for jj in range(8):
                        blk = half * 8 + jj
                        c, n = blk // nb, blk % nb
                        t0 = c + r * 128 * n
                        OP("pe", "transpose", [r_v, r_ident], [bankT[tb]], signal=(jj == 7),
                           out=psT[:, tb, jj * 128:(jj + 1) * 128], in_=vT[:, t0:t0 + 127 * r + 1:r], identity=ident_bf[:])
                    evac_copy(Vb[g][:, half * 8:(half + 1) * 8, :], psT[:, tb, :].rearrange("p (a b) -> p a b", b=128), [bankT[tb]], [r_Vb[g]])
            for g, (win, r) in enumerate(DIL):
                nb = SEQ // r // 128
                gh = g * 4 + h
                Eflat = Et[:, gh, :, :].rearrange("p a b -> p (a b)")
                for blk in range(16):
                    c, n = blk // nb, blk % nb
                    t0 = c + r * 128 * n
                    qsl = qT[g][:, t0:t0 + 127 * r + 1:r]
                    nkb = 2 if n > 0 else 1
                    bs, bo = 4, 5
                    pi = pcount % 2
                    so = pi * 256
                    pcount += 1
                    for kb in range(nkb):
                        tk = t0 - kb * r * 128
                        OP("pe", "matmul", [r_k, r_q[g]], [bankF[bs]], signal=(kb == nkb - 1),
                           out=psF[:, bs, so + kb * 128:so + (kb + 1) * 128], lhsT=kT[:, tk:tk + 127 * r + 1:r], rhs=qsl, start=True, stop=True)
                    w = nkb * 128
                    ACTV(Pf[pi][:, 0:w], psF[:, bs, so:so + w], AF.Exp, [bankF[bs]], [r_Pf[pi]])
                    OP("pool", "tensor_tensor", [r_Pf[pi], r_E], [r_Pb[pi]], out=Pb[pi][:, 0:w], in0=Pf[pi][:, 0:w], in1=Eflat[:, 0:w], op=ALU.mult)
                    for part in range(2):
                        for kb in range(nkb):
                            lhs = Vb[g][:, blk - kb, :] if part == 0 else ones_bf[:]
                            OP("pe", "matmul", [r_Vb[g], r_ones, r_Pb[pi]], [bankF[bo]], signal=(part == 1 and kb == nkb - 1),
                               out=psF[:, bo, so + part * 128:so + (part + 1) * 128], lhsT=lhs, rhs=Pb[pi][:, kb * 128:(kb + 1) * 128],
                               start=(kb == 0), stop=(kb == nkb - 1))
                    osl = acc[:, :, t0:t0 + 127 * r + 1:r]
                    psl = psF[:, bo, so:so + 256].rearrange("p (a b) -> p a b", b=128)
                    if g == 0:
                        OP("dve", "tensor_copy", [bankF[bo]], [r_acc], out=osl, in_=psl)
                    else:
                        OP("dve", "tensor_tensor", [bankF[bo], r_acc], [r_acc], out=osl, in0=osl, in1=psl, op=ALU.add)
            OP("dve", "reciprocal", [r_acc], [r_rcp], out=rcp[:], in_=acc[:, 1, :])
            OP("dve", "tensor_tensor", [r_acc, r_rcp], [r_oatt[h]], out=oatt[:, h, :], in0=acc[:, 0, :], in1=rcp[:], op=ALU.mult)
        phaseB_res = r_q + [r_k, r_v] + r_Vb + [r_acc] + r_Pf + r_Pb + [r_rcp]
        if stop_after == "B":
            break

        dC = phaseB_res + dB
        r_xr = Res("xr", after=dC)
        r_xc = Res("xc", after=dC)
        r_xcb = Res("xcb", after=dC)
        r_rr = Res("rr", after=dC)
        r_ii = Res("ii", after=dC)
        sq_ap = xr[:, 4:4 + SEQ]
        OP("pool", "memset", [], [r_xr], ap=xr[:, 0:4], constant=0.0)
        for c in range(NRC):
            si, (v0, v1, v2) = stage([(w_in_v[:, :, c * 128:(c + 1) * 128], [8, 128]), (w_rga_v[:, c, :], [128]), (w_rgx_v[:, c, :], [128])])
            wi = next_wbf()
            wt = wbf[wi][:, 0:1024].rearrange("p (a b) -> p a b", b=128)
            cast(si, v0, wt, r_wbf[wi], scale=gpre)
            cast(si, stg[si][:, 1024:1280], wbf[wi][:, 1024:1280], r_wbf[wi])
            wa = wbf[wi][:, 1024:1152]
            wx = wbf[wi][:, 1152:1280]
            proj_fm(wt, 8, hnT, hn_res,
                    lambda tg, b: ACTV(xr[:, 4 + tg * 512:4 + (tg + 1) * 512], psF[:, b, :], AF.Copy, [bankF[b]], [r_xr]),
                    r_wbf[wi])
            cw = [vecs[:, c * 4 + k:c * 4 + k + 1] for k in range(4)]
            OP("dve", "tensor_scalar", [r_xr, r_vecs], [r_xc], out=xc[:], in0=xr[:, 4:4 + SEQ], scalar1=cw[3], scalar2=vecs[:, 40 + c:41 + c], op0=ALU.mult, op1=ALU.add)
            for k in range(3):
                OP("dve", "scalar_tensor_tensor", [r_xr, r_vecs, r_xc], [r_xc], out=xc[:], in0=xr[:, 1 + k:1 + k + SEQ], scalar=cw[k], in1=xc[:], op0=ALU.mult, op1=ALU.add)
            OP("pool", "tensor_copy", [r_xc], [r_xcb], out=xcb[:], in_=xc[:])
            for tg in range(4):
                ba, bx = nextF(), nextF()
                sl = slice(tg * 512, (tg + 1) * 512)
                OP("pe", "matmul", [r_wbf[wi], r_xcb], [bankF[ba]], out=psF[:, ba, :], lhsT=wa, rhs=xcb[:, sl], start=True, stop=True)
                OP("pe", "matmul", [r_wbf[wi], r_xcb], [bankF[bx]], out=psF[:, bx, :], lhsT=wx, rhs=xcb[:, sl], start=True, stop=True)
                ACTV(rr[:, sl], psF[:, ba, :], AF.Sigmoid, [bankF[ba], r_vecs], [r_rr], bias=vecs[:, 50 + c:51 + c])
                ACTV(ii[:, sl], psF[:, bx, :], AF.Sigmoid, [bankF[bx], r_vecs], [r_ii], bias=vecs[:, 60 + c:61 + c])
            ACTV(sq_ap, rr[:], AF.Exp, [r_rr, r_cf], [r_xr], scale=cf[:, 10 + c:11 + c])
            ACTV(sq_ap, sq_ap, AF.Sqrt, [r_xr], [r_xr], scale=-1.0, bias=1.0)
            ACTV(rr[:], rr[:], AF.Exp, [r_rr, r_cf], [r_rr], scale=cf[:, c:c + 1])
            OP("pool", "tensor_tensor", [r_ii, r_xc], [r_ii], out=ii[:], in0=ii[:], in1=xc[:], op=ALU.mult)
            OP("pool", "tensor_tensor", [r_ii, r_xr], [r_ii], out=ii[:], in0=ii[:], in1=sq_ap, op=ALU.mult)
            OP("dve", "tensor_tensor_scan", [r_rr, r_ii], [r_yrnn[c]], out=yrnn[:, c, :], data0=rr[:], data1=ii[:], initial=0.0, op0=ALU.mult, op1=ALU.add)
        phaseC_res = [r_xr, r_xc, r_xcb, r_rr, r_ii]

        if debug and s == 0:
            for nm, t, rl in [("d_hnT", hnT, r_hnT), ("d_yrnn", yrnn, r_yrnn), ("d_oatt", oatt, r_oatt)]:
                for cc in range(t.shape[1]):
                    OP("dve", "tensor_copy", list(rl) + [r_xc], [r_xc], out=xc[:], in_=t[:, cc, :])
                    dbg_toks.append(DMA(dbg[nm][:, cc * SEQ:(cc + 1) * SEQ], xc[:], reads=[r_xc], key="dbg"))
        if stop_after == "C":
            break

        dD = phaseC_res + dC
        r_tm = [Res("tm%d" % i, after=dD) for i in range(2)]
        for m in range(8):
            wi = next_wbf()
            wv = wbf[wi][:, 0:30 * 128].rearrange("p (a b) -> p a b", b=128)
            cs = slice(m * 128, (m + 1) * 128)
            si, (v0, v1) = stage([(w_in_v[:, :, OFF_GR + m * 128:OFF_GR + (m + 1) * 128], [8, 128]), (w_in_v[:, :, OFF_GA + m * 128:OFF_GA + (m + 1) * 128], [8, 128])])
            cast(si, v0, wv[:, 0:8, :], r_wbf[wi], scale=gpre)
            cast(si, v1, wv[:, 8:16, :], r_wbf[wi], scale=gpre)
            si, (v0, v1) = stage([(w_br_v[:, :, cs], [10, 128]), (w_ba_v[:, :, cs], [4, 128])])
            cast(si, stg[si][:, 0:14 * 128], wbf[wi][:, 16 * 128:30 * 128], r_wbf[wi])
            for tg in range(4):
                sl = slice(tg * 512, (tg + 1) * 512)
                bgr, bga, bbr, bba = nextF(), nextF(), nextF(), nextF()
                for (b, k0, nk, rhs_t, rres) in ((bgr, 0, 8, hnT, hn_res(tg)), (bga, 8, 8, hnT, hn_res(tg)), (bbr, 16, 10, yrnn, r_yrnn), (bba, 26, 4, oatt, r_oatt)):
                    for kc in range(nk):
                        OP("pe", "matmul", [r_wbf[wi]] + list(rres), [bankF[b]], signal=(kc == nk - 1),
                           out=psF[:, b, :], lhsT=wv[:, k0 + kc, :], rhs=rhs_t[:, kc, sl], start=(kc == 0), stop=(kc == nk - 1))
                ACTV(tm[0][:], psF[:, bgr, :], AF.Sigmoid, [bankF[bgr]], [r_tm[0]])
                ACTV(tm[1][:], psF[:, bga, :], AF.Sigmoid, [bankF[bga]], [r_tm[1]])
                OP("dve", "tensor_tensor", [r_tm[0], bankF[bbr]], [r_tm[0]], out=tm[0][:], in0=tm[0][:], in1=psF[:, bbr, :], op=ALU.mult)
                OP("dve", "tensor_tensor", [r_tm[1], bankF[bba]], [r_tm[1]], out=tm[1][:], in0=tm[1][:], in1=psF[:, bba, :], op=ALU.mult)
                OP("pool", "tensor_tensor", [r_tm[0], r_tm[1]], [r_merged[m]], out=merged[:, m, sl], in0=tm[0][:], in1=tm[1][:], op=ALU.add)
        if debug and s == 0:
            r_dbuf = Res("dbuf", after=list(r_yrnn) + list(r_oatt))
            for cc in range(8):
                OP("dve", "tensor_copy", list(r_merged) + [r_dbuf], [r_dbuf], out=dbuf[:], in_=merged[:, cc, :])
                dbg_toks.append(DMA(dbg["d_merged"][:, cc * SEQ:(cc + 1) * SEQ], dbuf[:], reads=[r_dbuf], key="dbg"))
            dD = dD + [r_dbuf]
        if stop_after == "D1":
            break

        dD2 = list(r_tm) + dD + list(r_yrnn) + list(r_oatt)
        r_wdn = [Res("wdn%d" % i, after=dD2) for i in range(12)]
        r_xt2 = Res("xt2", after=dD2)
        r_ht = [Res("ht%d" % i, after=dD2) for i in range(2)]
        r_hn2b = Res("hn2b", after=dD2)
        r_junk2 = Res("junk2", after=dD2)
        r_st2 = Res("st2")
        r_st2b = Res("st2b")
        r_hn2T = [Res("hn2T%d" % t, after=[r_hnT[t]]) for t in range(16)]
        r_hrow = [Res("hrow%d" % t) for t in range(16)]
        for tt in range(16):
            if tt < 12:
                si, (v,) = stage([(w_d_v[:, 2 * tt:2 * tt + 2, :], [2, D])])
                cast(si, v, wdn[:, 2 * tt:2 * tt + 2, :], r_wdn[tt])
            DMA(xt2[:], x_d[row0 + tt * 128:row0 + (tt + 1) * 128, :], writes=[r_xt2], key="xt2")
            p = nextP()
            for half in range(2):
                b = 2 * p + half
                for kc in range(8):
                    OP("pe", "matmul", list(r_merged) + [r_wout], [bankF[b]], signal=(kc == 7),
                       out=psF[:, b, :], lhsT=merged[:, kc, tt * 128:(tt + 1) * 128], rhs=w_out_bf[:, kc, half * 512:(half + 1) * 512], start=(kc == 0), stop=(kc == 7))
            mix = psF[:, 2 * p:2 * p + 2, :].rearrange("p a b -> p (a b)")
            pb = [bankF[2 * p], bankF[2 * p + 1]]
            hb = tt % 2
            ss, sq, rs = stats[:, 16:17], stats[:, 17:18], stats[:, 18:19]
            ACTV(junk2[:], mix, AF.Square, pb, [r_junk2, r_st2], accum_out=ss)
            ACTV(sq, ss, AF.Sqrt, [r_st2], [r_st2], scale=1.0 / D, bias=EPS)
            OP("dve", "reciprocal", [r_st2], [r_st2], out=rs, in_=sq)
            OP("dve", "scalar_tensor_tensor", pb + [r_st2, r_gpost], [r_ht[hb]], out=ht[hb][:], in0=mix, scalar=rs, in1=gpost[:], op0=ALU.mult, op1=ALU.mult)
            OP("pool", "tensor_tensor", [r_ht[hb], r_xt2], [r_ht[hb]], out=ht[hb][:], in0=ht[hb][:], in1=xt2[:], op=ALU.add)
            DMA(out_d[row0 + tt * 128:row0 + (tt + 1) * 128, :], ht[hb][:], reads=[r_ht[hb]], writes=[r_hrow[tt]], key="hst%d" % hb)
            norm_to_T(ht[hb][:], r_ht[hb], hn2b[:], r_hn2b, junk2[:], r_junk2, r_st2b, hnT, r_hn2T[tt], tt, 20)
        phaseD2_res = [r_xt2, r_ht[0], r_ht[1], r_hn2b, r_junk2]
        if stop_after == "D2":
            out_toks = [r.w for r in r_hrow]
            break

        dE_ = phaseD2_res + dD2 + list(r_merged)
        r_gt = [Res("gt%d" % i, after=dE_) for i in range(2)]
        r_gc = [Res("gc%d" % i, after=dE_) for i in range(2)]
        r_ge = [Res("ge%d" % i, after=dE_) for i in range(2)]
        r_ot = Res("ot", after=dE_)
        r_hbuf = Res("hbuf", after=dE_)
        r_junk3 = Res("junk3", after=dE_)
        r_carry = Res("carry")
        r_st3 = Res("st3")
        r_ffT = [Res("ffT%d" % c, after=dE_) for c in range(NFC)]
        hn2_res = lambda tg: r_hn2T[tg * 4:(tg + 1) * 4]
        for tg in range(4):
            sl = slice(tg * 512, (tg + 1) * 512)
            for c in range(NFC):
                bi = c % 2
                si, (v0, v1) = stage([(w_g_v[:, :, c * 128:(c + 1) * 128], [8, 128]), (w_u_v[:, :, c * 128:(c + 1) * 128], [8, 128])])
                wi = next_wbf()
                wv = wbf[wi][:, 0:2048].rearrange("p (a b) -> p a b", b=128)
                cast(si, v0, wv[:, 0:8, :], r_wbf[wi], scale=gpre2)
                cast(si, v1, wv[:, 8:16, :], r_wbf[wi], scale=gpre2)
                bg, bu = nextF(), nextF()
                for (b, k0) in ((bg, 0), (bu, 8)):
                    for kc in range(8):
                        OP("pe", "matmul", [r_wbf[wi]] + hn2_res(tg), [bankF[b]], signal=(kc == 7),
                           out=psF[:, b, :], lhsT=wv[:, k0 + kc, :], rhs=hnT[:, kc, sl], start=(kc == 0), stop=(kc == 7))
                if tg == 0:
                    OP("pool", "memset", [], [r_gt[bi]], ap=gt[bi][:, 0:4], constant=0.0)
                else:
                    OP("pool", "tensor_copy", [r_carry], [r_gt[bi]], out=gt[bi][:, 2:4], in_=carry[:, c, :])
                ACTV(gt[bi][:, 4:516], psF[:, bg, :], AF.Copy, [bankF[bg]], [r_gt[bi]])
                if tg < 3:
                    OP("pool", "tensor_copy", [r_gt[bi]], [r_carry], out=carry[:, c, :], in_=gt[bi][:, 514:516])
                fw = [vecs[:, 80 + c * 3 + k:80 + c * 3 + k + 1] for k in range(3)]
                OP("dve", "tensor_scalar", [r_gt[bi], r_vecs], [r_gc[bi]], out=gc[bi][:], in0=gt[bi][:, 4:516], scalar1=fw[2], scalar2=vecs[:, 152 + c:153 + c], op0=ALU.mult, op1=ALU.add)
                for k in range(2):
                    OP("dve", "scalar_tensor_tensor", [r_gt[bi], r_vecs, r_gc[bi]], [r_gc[bi]], out=gc[bi][:], in0=gt[bi][:, 2 + k:2 + k + 512], scalar=fw[k], in1=gc[bi][:], op0=ALU.mult, op1=ALU.add)
                ACTV(ge[bi][:], gc[bi][:], AF.Gelu_apprx_tanh, [r_gc[bi]], [r_ge[bi]])
                OP("dve", "tensor_tensor", [r_ge[bi], bankF[bu]], [r_ffT[c]], out=ffT[:, c, :], in0=ge[bi][:], in1=psF[:, bu, :], op=ALU.mult)
            for t4 in range(4):
                tt = tg * 4 + t4
                DMA(hbuf[:], out_d[row0 + tt * 128:row0 + (tt + 1) * 128, :], reads=[r_hrow[tt]], writes=[r_hbuf], key="hbuf")
                p = nextP()
                for half in range(2):
                    b = 2 * p + half
                    for kc in range(NFC):
                        OP("pe", "matmul", list(r_ffT) + list(r_wdn), [bankF[b]], signal=(kc == NFC - 1),
                           out=psF[:, b, :], lhsT=ffT[:, kc, t4 * 128:(t4 + 1) * 128], rhs=wdn[:, kc, half * 512:(half + 1) * 512], start=(kc == 0), stop=(kc == NFC - 1))
                ff = psF[:, 2 * p:2 * p + 2, :].rearrange("p a b -> p (a b)")
                pb = [bankF[2 * p], bankF[2 * p + 1]]
                ss, sq, rs = stats[:, 24:25], stats[:, 25:26], stats[:, 26:27]
                ACTV(junk3[:], ff, AF.Square, pb, [r_junk3, r_st3], accum_out=ss)
                ACTV(sq, ss, AF.Sqrt, [r_st3], [r_st3], scale=1.0 / D, bias=EPS)
                OP("dve", "reciprocal", [r_st3], [r_st3], out=rs, in_=sq)
                OP("dve", "scalar_tensor_tensor", pb + [r_st3, r_gpost2], [r_ot], out=ot[:], in0=ff, scalar=rs, in1=gpost2[:], op0=ALU.mult, op1=ALU.mult)
                OP("pool", "tensor_tensor", [r_ot, r_hbuf], [r_ot], out=ot[:], in0=ot[:], in1=hbuf[:], op=ALU.add)
                out_toks.append(DMA(out_d[row0 + tt * 128:row0 + (tt + 1) * 128, :], ot[:], reads=[r_ot], writes=[r_hrow[tt]], key="ost"))
        base_dead = r_gt + r_gc + r_ge + [r_ot, r_hbuf, r_junk3] + list(r_ffT) + list(r_wdn)
        prev_hn2T = r_hn2T

    S.wait_tokens("sp", out_toks + dbg_toks)
    run_block(nc, S)
    SCHED_STATS.clear()
    SCHED_STATS.update({e: len(S.ops[e]) for e in ENGS})
    return nc


def host_consts(inp):
    f = lambda a: np.ascontiguousarray(np.asarray(a, dtype=np.float32))
    vecs = np.zeros((128, NV), np.float32)
    crw = f(inp["conv_rnn_w"])[0]
    vecs[:, 0:40] = crw.reshape(4, NRC, 128).transpose(2, 1, 0).reshape(128, 40)
    vecs[:, 40:50] = f(inp["conv_rnn_b"])[0].reshape(NRC, 128).T
    vecs[:, 50:60] = f(inp["b_rg_a"])[0].reshape(NRC, 128).T
    vecs[:, 60:70] = f(inp["b_rg_x"])[0].reshape(NRC, 128).T
    vecs[:, 70:80] = f(inp["lru_lambda"])[0].reshape(NRC, 128).T
    cfw = f(inp["conv_ffn_w"])[0]
    vecs[:, 80:152] = cfw.reshape(3, NFC, 128).transpose(2, 1, 0).reshape(128, 72)
    vecs[:, 152:176] = f(inp["conv_ffn_b"])[0].reshape(NFC, 128).T
    vecs[:, 176:184] = f(inp["norm_mix_pre"])[0].reshape(8, 128).T
    vecs[:, 184:192] = f(inp["norm_ffn_pre"])[0].reshape(8, 128).T
    oh = np.zeros((32, 3 * 129), np.float32)
    for g, (win, r) in enumerate(DIL):
        bk = _t5_bucket(np.arange(129) * r)
        oh[bk, g * 129 + np.arange(129)] = 1.0
    shared = {
        "w_in": f(inp["w_in"])[0],
        "w_rg_a": f(inp["w_rg_a"])[0].reshape(NRC * 128, 128),
        "w_rg_x": f(inp["w_rg_x"])[0].reshape(NRC * 128, 128),
        "w_branch_rnn": f(inp["w_branch_rnn"])[0],
        "w_branch_att": f(inp["w_branch_att"])[0],
        "w_out": f(inp["w_out"])[0],
        "w_ffn_gate": f(inp["w_ffn_gate"])[0],
        "w_ffn_up": f(inp["w_ffn_up"])[0],
        "w_ffn_down": f(inp["w_ffn_down"])[0],
        "rel_bias": f(inp["rel_bias"]),
        "oh": oh,
        "vecs": vecs,
        "gpost": np.ascontiguousarray(np.broadcast_to(f(inp["norm_mix_post"])[0], (128, D))),
        "gpost2": np.ascontiguousarray(np.broadcast_to(f(inp["norm_ffn_post"])[0], (128, D))),
        "ident": np.eye(128, dtype=np.float32),
        "jmat": np.ascontiguousarray(np.eye(128, dtype=np.float32)[::-1]),
    }
    return shared


_NC_CACHE = {}


def kernel(**inputs):
    x = np.asarray(inputs["x"], dtype=np.float32)
    B = x.shape[0]
    nseq = B // NCORES
    shared = host_consts(inputs)
    if nseq not in _NC_CACHE:
        _NC_CACHE[nseq] = build(nseq)
    nc = _NC_CACHE[nseq]
    in_maps = []
    for c in range(NCORES):
        m = dict(shared)
        m["x"] = np.ascontiguousarray(x[c * nseq:(c + 1) * nseq].reshape(nseq * SEQ, D))
        in_maps.append(m)
    res = run_bass_kernel_spmd(nc, in_maps, core_ids=list(range(NCORES)))
    outs = [np.asarray(r["out"]).reshape(nseq, SEQ, D) for r in res.results]
    return np.concatenate(outs, axis=0).astype(np.float32)
```

```python
import contextlib
import math
import numpy as np
import concourse.bass as bass
import concourse.mybir as mybir
from concourse.bass_utils import run_bass_kernel_spmd

F32 = mybir.dt.float32
BF16 = mybir.dt.bfloat16
AF = mybir.ActivationFunctionType
ALU = mybir.AluOpType

ENGS = ["pe", "act", "dve", "pool", "sp"]
EPOCH_LIMIT = 12000


class Res:
    __slots__ = ("name", "w", "r", "const")

    def __init__(self, name, after=(), const=False):
        self.name = name
        self.w = None
        self.r = {}
        self.const = const
        for o in after:
            if o.w is not None:
                self._addr(o.w)
            for t in o.r.values():
                self._addr(t)

    def _addr(self, t):
        k = t[1]
        o = self.r.get(k)
        if o is None or o[2] < t[2]:
            self.r[k] = t


class Sched:
    def __init__(self):
        self.ops = {e: [] for e in ENGS}
        self.count = {e: 0 for e in ENGS}
        self.epoch = {e: 0 for e in ENGS}
        self.known = {e: {} for e in ENGS}
        self.prev_epochs = {e: {} for e in ENGS}
        self.dma_cnt = {}
        self.pend_r = {e: [] for e in ENGS}
        self.pend_w = {e: [] for e in ENGS}
        self.semkeys = set()
        self.label = ''
        self.labels = {e: [] for e in ENGS}

    def _need(self, eng, tok, waits):
        key, val, snap = tok[1], tok[2], tok[3]
        kn = self.known[eng]
        if kn.get(key, 0) >= val:
            return
        waits.append((key, val))
        kn[key] = val
        for k, v in snap.items():
            if kn.get(k, 0) < v:
                kn[k] = v

    def op(self, eng, fn, reads=(), writes=(), dma=None, signal=True):
        waits = []
        isdma = dma is not None
        for r in reads:
            t = r.w
            if t is not None:
                if t[0] == eng and not isdma and eng == "pe":
                    continue
                self._need(eng, t, waits)
        strict = isdma or eng == "pool"
        for w in writes:
            t = w.w
            if t is not None and (t[0] != eng or strict):
                self._need(eng, t, waits)
            for t in w.r.values():
                if t[0] != eng or strict:
                    self._need(eng, t, waits)
        if isdma:
            key = ("dma", dma)
            ndma = len(fn) if isinstance(fn, list) else 1
            self.dma_cnt[key] = self.dma_cnt.get(key, 0) + 16 * ndma
            val = self.dma_cnt[key]
            tok = ("dma", key, val, dict(self.known[eng]))
            inc = (key, 16)
            self.semkeys.add(key)
        elif signal:
            if self.count[eng] >= EPOCH_LIMIT:
                self.prev_epochs[eng][(eng, self.epoch[eng])] = self.count[eng]
                self.epoch[eng] += 1
                self.count[eng] = 0
            self.count[eng] += 1
            key = (eng, self.epoch[eng])
            val = self.count[eng]
            snap = dict(self.known[eng])
            snap.update(self.prev_epochs[eng])
            tok = (eng, key, val, snap)
            inc = (key, 1)
            self.semkeys.add(key)
        else:
            tok = None
            inc = None
        if tok is None:
            self.pend_r[eng].extend(r for r in reads if not r.const)
            self.pend_w[eng].extend(writes)
        else:
            if not isdma:
                for r in self.pend_r[eng]:
                    r._addr(tok)
                for w in self.pend_w[eng]:
                    w.w = tok
                    w.r = {}
                self.pend_r[eng] = []
                self.pend_w[eng] = []
            for r in reads:
                if not r.const:
                    r._addr(tok)
            for w in writes:
                w.w = tok
                w.r = {}
        self.ops[eng].append((waits, fn, inc))
        self.labels[eng].append(self.label)
        return tok

    def wait_tokens(self, eng, toks):
        waits = []
        for t in toks:
            if t is not None:
                self._need(eng, t, waits)
        if waits:
            self.ops[eng].append((waits, None, None))


def _replay(sched, eng, e, sems):
    for waits, fn, inc in sched.ops[eng]:
        for key, val in waits:
            e.wait_ge(sems[key], val)
        if fn is None:
            continue
        if isinstance(fn, list):
            for meth, kw in fn:
                getattr(e, meth)(**kw).then_inc(sems[inc[0]], 16)
            continue
        meth, kw = fn
        ins = getattr(e, meth)(**kw)
        if inc is not None:
            ins.then_inc(sems[inc[0]], inc[1])


def run_block(nc, sched):
    with contextlib.ExitStack() as st:
        sems = {}
        for i, key in enumerate(sorted(sched.semkeys, key=str)):
            sems[key] = st.enter_context(nc.semaphore("s%d" % i))
        block = st.enter_context(nc.Block())

        @block.tensor
        def _(e):
            _replay(sched, "pe", e, sems)

        @block.scalar
        def _(e):
            _replay(sched, "act", e, sems)

        @block.vector
        def _(e):
            _replay(sched, "dve", e, sems)

        @block.gpsimd
        def _(e):
            _replay(sched, "pool", e, sems)

        @block.sync
        def _(e):
            _replay(sched, "sp", e, sems)


D = 1024
SEQ = 2048
NCORES = 8
RNN_W = 1280
NRC = 10
HD = 128
NH = 4
DIL = ((128, 1), (512, 4), (2048, 16))
FFN = 3072
NFC = 24
EPS = 1e-6
IN_W = 5888
OFF_Q = 1280
OFF_K = OFF_Q + 1536
OFF_V = OFF_K + 512
OFF_GR = OFF_V + 512
OFF_GA = OFF_GR + 1024
NEG = -30000.0
NV = 192
SM_SCALE = HD ** -0.5
SCHED_STATS = {}


def _t5_bucket(dist):
    max_exact = 16
    d = np.maximum(dist, 1).astype(np.float32)
    large = max_exact + np.log(d / max_exact) / math.log(2048 / max_exact) * (32 - max_exact)
    large = np.minimum(large.astype(np.int32), 31)
    return np.where(dist < max_exact, dist, large).astype(np.int32)


def bcast_free(ap2d, n):
    return bass.AP(ap2d.tensor, ap2d.offset, [list(ap2d.ap[0]), list(ap2d.ap[1]), [0, n]])


def build(nseq, debug=False, stop_after=None):
    nc = bass.Bass("TRN2", target_bir_lowering=False)
    T = nseq * SEQ

    def dram(name, shape, kind="ExternalInput"):
        return nc.dram_tensor(name, list(shape), F32, kind=kind).ap()

    x_d = dram("x", [T, D])
    out_d = dram("out", [T, D], "ExternalOutput")
    w_in_d = dram("w_in", [D, IN_W])
    w_rga_d = dram("w_rg_a", [NRC * 128, 128])
    w_rgx_d = dram("w_rg_x", [NRC * 128, 128])
    w_br_d = dram("w_branch_rnn", [RNN_W, D])
    w_ba_d = dram("w_branch_att", [512, D])
    w_out_d = dram("w_out", [D, D])
    w_g_d = dram("w_ffn_gate", [D, FFN])
    w_u_d = dram("w_ffn_up", [D, FFN])
    w_d_d = dram("w_ffn_down", [FFN, D])
    relb_d = dram("rel_bias", [32, 12])
    oh_d = dram("oh", [32, 3 * 129])
    vecs_d = dram("vecs", [128, NV])
    gpost_d = dram("gpost", [128, D])
    gpost2_d = dram("gpost2", [128, D])
    ident_d = dram("ident", [128, 128])
    jmat_d = dram("jmat", [128, 128])
    sc_d = nc.dram_tensor("scratch_ext", [12, 383], F32).ap()
    dbg = {}
    if debug:
        for nm, shp in [("d_hnT", [128, 8 * SEQ]), ("d_yrnn", [128, NRC * SEQ]), ("d_oatt", [128, 4 * SEQ]),
                        ("d_merged", [128, 8 * SEQ]), ("d_E", [128, 12 * 256]), ("d_cfac", [128, 16])]:
            dbg[nm] = dram(nm, shp, "ExternalOutput")

    w_in_v = w_in_d.rearrange("(kc p) n -> p kc n", p=128)
    w_g_v = w_g_d.rearrange("(kc p) n -> p kc n", p=128)
    w_u_v = w_u_d.rearrange("(kc p) n -> p kc n", p=128)
    w_br_v = w_br_d.rearrange("(kc p) n -> p kc n", p=128)
    w_ba_v = w_ba_d.rearrange("(kc p) n -> p kc n", p=128)
    w_out_v = w_out_d.rearrange("(kc p) n -> p kc n", p=128)
    w_d_v = w_d_d.rearrange("(kc p) n -> p kc n", p=128)
    w_rga_v = w_rga_d.rearrange("(c p) n -> p c n", p=128)
    w_rgx_v = w_rgx_d.rearrange("(c p) n -> p c n", p=128)

    BASE = 24576
    LIMIT = 229312
    cur = [BASE]

    def take(nb):
        o = cur[0]
        cur[0] += (nb + 63) // 64 * 64
        assert cur[0] <= LIMIT, "SBUF overflow %d" % cur[0]
        return o

    cnt = [0]

    def sbt(shape, dt, off):
        cnt[0] += 1
        return nc.alloc_sbuf_tensor_at("t%d" % cnt[0], list(shape), dt, offset=off)

    def nbytes(shape, dt):
        return int(np.prod(shape[1:])) * (4 if dt == F32 else 2)

    def new(shape, dt):
        return sbt(shape, dt, take(nbytes(shape, dt)))

    ident_bf = new([128, 128], BF16)
    ones_bf = new([128, 128], BF16)
    vecs = new([128, NV], F32)
    cf = new([128, 32], F32)
    Et = new([128, 12, 2, 128], BF16)
    gpost = new([128, D], F32)
    gpost2 = new([128, D], F32)
    w_out_bf = new([128, 8, D], BF16)
    stats = new([128, 64], F32)
    carry = new([128, NFC, 2], F32)
    hcar = new([128, 16], F32)
    stg = [new([128, 2048], F32) for _ in range(2)]
    wbf = [new([128, 2048], BF16) for _ in range(4)]
    R1_off = take(8 * SEQ * 2)
    R2_off = take(14 * SEQ * 2)
    R3_off = take(8 * SEQ * 2)
    W_off = cur[0]
    W_size = LIMIT - W_off
    R3_size = 8 * SEQ * 2

    hnT = sbt([128, 8, SEQ], BF16, R1_off)
    yrnn = sbt([128, NRC, SEQ], BF16, R2_off)
    oatt = sbt([128, 4, SEQ], BF16, R2_off + NRC * SEQ * 2)
    wdn = sbt([128, NFC, D], BF16, R2_off)
    R2_spare = R2_off + NFC * D * 2
    merged = sbt([128, 8, SEQ], BF16, R3_off)
    ffT = sbt([128, NFC, 512], BF16, R3_off)
    R3_spare = R3_off + NFC * 512 * 2

    class Region:
        def __init__(self, spans):
            self.spans = [list(sp) for sp in spans]
            self.i = 0
            self.o = self.spans[0][0]

        def new(self, shape, dt):
            nb = (nbytes(shape, dt) + 63) // 64 * 64
            while True:
                sp = self.spans[self.i]
                if self.o + nb <= sp[0] + sp[1]:
                    o = self.o
                    self.o += nb
                    return sbt(shape, dt, o)
                self.i += 1
                assert self.i < len(self.spans), "region overflow"
                self.o = self.spans[self.i][0]

    WR = Region([(W_off, W_size), (R3_off, R3_size)])
    tmpc = WR.new([128, 64], F32)
    rb = WR.new([32, 12], F32)
    oh = WR.new([32, 3 * 129], F32)
    ext = WR.new([12, 3, 383], F32)
    jm = WR.new([128, 128], F32)
    Hb = [WR.new([128, 256], F32) for _ in range(2)]
    dE = WR.new([128, 12 * 256], F32) if debug else None

    RA = Region([(R3_off, R3_size)])
    xt = [RA.new([128, D], F32) for _ in range(2)]
    hnb = [RA.new([128, D], BF16) for _ in range(2)]
    junkA = RA.new([128, D], BF16)

    RB = Region([(R3_off, R3_size), (R2_off, NRC * SEQ * 2), (W_off, W_size)])
    qT = [RB.new([128, SEQ], BF16) for _ in range(3)]
    kT = RB.new([128, SEQ], BF16)
    vT = RB.new([128, SEQ], BF16)
    Vb = [RB.new([128, 16, 128], BF16) for _ in range(3)]
    acc = RB.new([128, 2, SEQ], F32)
    Pf = [RB.new([128, 256], F32) for _ in range(2)]
    Pb = [RB.new([128, 256], BF16) for _ in range(2)]
    rcp = RB.new([128, SEQ], F32)

    HS = SEQ // 2
    RC = Region([(R3_off, R3_size), (W_off, W_size)])
    xr_h = [RC.new([128, HS + 4], F32) for _ in range(2)]
    xc_h = [RC.new([128, HS], F32) for _ in range(2)]
    xcb_h = [RC.new([128, HS], BF16) for _ in range(2)]
    rr_h = [RC.new([128, HS], F32) for _ in range(2)]
    ii_h = [RC.new([128, HS], F32) for _ in range(2)]
    dbufC = RC.new([128, SEQ], F32) if debug else None

    dbuf = sbt([128, SEQ], F32, R2_off)
    RD = Region([(W_off, W_size)])
    tm = [[RD.new([128, 512], F32) for _ in range(2)] for _ in range(2)]

    RD2 = Region([(W_off, W_size), (R2_spare, 8192)])
    ht = [RD2.new([128, D], F32) for _ in range(2)]
    xt2 = [RD2.new([128, D], F32) for _ in range(2)]
    hn2b = [RD2.new([128, D], BF16) for _ in range(2)]
    junk2 = RD2.new([128, D], BF16)

    RE = Region([(W_off, W_size), (R3_spare, 8192), (R2_spare, 8192)])
    gt = [RE.new([128, 516], F32) for _ in range(2)]
    gc = [RE.new([128, 512], F32) for _ in range(2)]
    ge = [RE.new([128, 512], F32) for _ in range(2)]
    ot = RE.new([128, D], F32)
    hbuf = RE.new([128, D], F32)
    junk3 = RE.new([128, D], BF16)

    psF = nc.alloc_psum_tensor("psF", [128, 8, 512], F32)
    psT = psF[:, 6:8, :].bitcast(BF16)
    bankF = [Res("bankF%d" % i) for i in range(8)]
    bankT = [bankF[6], bankF[7]]
    rot = {"F": 0, "T": 0, "P": 0, "modF": 6}

    def nextF():
        b = rot["F"] % rot["modF"]
        rot["F"] = (b + 1) % rot["modF"]
        return b

    def nextT():
        b = rot["T"]
        rot["T"] = (b + 1) % 2
        return b

    def nextP():
        b = rot["P"]
        rot["P"] = (b + 1) % 3
        return b

    S = Sched()
    dbg_toks = []

    def OP(eng, meth, reads=(), writes=(), signal=True, **kw):
        return S.op(eng, (meth, kw), reads=reads, writes=writes, signal=signal)

    def DMA(out, in_, reads=(), writes=(), key=None):
        return S.op("sp", [("dma_start", dict(out=out, in_=in_))], reads=reads, writes=writes, dma=key)

    def ACTV(out, in_, func, reads, writes, **kw):
        return OP("act", "activation", reads, writes, out=out, in_=in_, func=func, **kw)

    CR = lambda n: Res(n, const=True)
    r_ident, r_ones, r_vecs, r_cf, r_E = CR("ident"), CR("ones"), CR("vecs"), CR("cf"), CR("E")
    r_gpost, r_gpost2, r_wout = CR("gpost"), CR("gpost2"), CR("wout")
    r_stg = [Res("stg0"), Res("stg1")]
    r_wbf = [Res("wbf%d" % i) for i in range(4)]

    class WPipe:
        LA = 2

        def __init__(self):
            self.units = []
            self.ns = 0
            self.ncast = 0
            self.nslot = 0

        def add(self, pieces, fixed=None):
            n = sum(int(np.prod(sh)) for _, sh in pieces)
            assert n <= 2048
            self.units.append(dict(pieces=pieces, n=n, fixed=fixed, slot=None))
            return len(self.units) - 1

        def _stage(self, j):
            u = self.units[j]
            si = j % 2
            o = 0
            calls = []
            for src, shape in u["pieces"]:
                n = int(np.prod(shape))
                if len(shape) == 2:
                    dst = stg[si][:, o:o + n].rearrange("p (a b) -> p a b", b=shape[1])
                else:
                    dst = stg[si][:, o:o + n]
                calls.append(("dma_start", dict(out=dst, in_=src)))
                o += n
            S.op("sp", calls, writes=[r_stg[si]], dma="stg%d" % si)

        def _cast(self, j):
            u = self.units[j]
            si = j % 2
            n = u["n"]
            if u["fixed"] is not None:
                dst, dres = u["fixed"]
            else:
                slot = self.nslot % 4
                self.nslot += 1
                u["slot"] = slot
                dst, dres = wbf[slot][:, 0:n], r_wbf[slot]
            ACTV(dst, stg[si][:, 0:n], AF.Copy, [r_stg[si]], [dres])

        def ensure(self, k):
            N = len(self.units)
            tc = min(k + 1 + self.LA, N)
            while True:
                if self.ns < N and self.ns < self.ncast + 2:
                    self._stage(self.ns)
                    self.ns += 1
                elif self.ncast < tc and not self.units[self.ncast].get("hold"):
                    self._cast(self.ncast)
                    self.ncast += 1
                else:
                    break

        def get(self, k):
            self.ensure(k)
            sl = self.units[k]["slot"]
            return wbf[sl], r_wbf[sl]

    WP = WPipe()
    u_ident = WP.add([(ident_d, [128])], fixed=(ident_bf[:], r_ident))
    u_wout = [WP.add([(w_out_v[:, :, q4 * 256:(q4 + 1) * 256], [8, 256])],
                     fixed=(w_out_bf[:, :, q4 * 256:(q4 + 1) * 256], r_wout)) for q4 in range(4)]

    useq = []
    r_wdn_all = []
    for s in range(nseq):
        us = {}
        us["B"] = []
        for h in range(NH):
            cq = [OFF_Q + (j * 4 + h) * 128 for j in range(3)]
            ck, cv = OFF_K + h * 128, OFF_V + h * 128
            wsl = lambda c0: (w_in_v[:, :, c0:c0 + 128], [8, 128])
            us["B"].append([WP.add([wsl(cq[0]), wsl(cq[1])]), WP.add([wsl(cq[2]), wsl(ck)]), WP.add([wsl(cv)])])
        us["C"] = [WP.add([(w_in_v[:, :, c * 128:(c + 1) * 128], [8, 128]), (w_rga_v[:, c, :], [128]), (w_rgx_v[:, c, :], [128])]) for c in range(NRC)]
        us["D1"] = []
        for m in range(8):
            cs = slice(m * 128, (m + 1) * 128)
            a = WP.add([(w_in_v[:, :, OFF_GR + m * 128:OFF_GR + (m + 1) * 128], [8, 128]), (w_in_v[:, :, OFF_GA + m * 128:OFF_GA + (m + 1) * 128], [8, 128])])
            b = WP.add([(w_br_v[:, :, cs], [10, 128]), (w_ba_v[:, :, cs], [4, 128])])
            us["D1"].append((a, b))
        rw = [Res("wdn%d_%d" % (s, i)) for i in range(12)]
        r_wdn_all.append(rw)
        us["Wd"] = [WP.add([(w_d_v[:, 2 * t:2 * t + 2, :], [2, D])], fixed=(wdn[:, 2 * t:2 * t + 2, :].rearrange("p a b -> p (a b)"), rw[t])) for t in range(12)]
        for ui_ in us["Wd"]:
            WP.units[ui_]["hold"] = True
        us["E"] = [[WP.add([(w_g_v[:, :, c * 128:(c + 1) * 128], [8, 128]), (w_u_v[:, :, c * 128:(c + 1) * 128], [8, 128])]) for c in range(NFC)] for tg in range(4)]
        useq.append(us)
    for q4, uidx in enumerate(u_wout):
        WP.units[uidx]["fixed"] = None
        WP.units[uidx]["wout_q4"] = q4

    _orig_cast = WP._cast

    def _cast2(j):
        u = WP.units[j]
        if "wout_q4" in u:
            q4 = u["wout_q4"]
            si = j % 2
            ACTV(w_out_bf[:, :, q4 * 256:(q4 + 1) * 256], stg[si][:, 0:2048].rearrange("p (a b) -> p a b", b=256), AF.Copy, [r_stg[si]], [r_wout])
        else:
            _orig_cast(j)
    WP._cast = _cast2

    gpre = vecs[:, 176:184]
    gpre2 = vecs[:, 184:192]
    DMA(vecs[:], vecs_d, writes=[r_vecs], key="c_vecs")
    DMA(gpost[:], gpost_d, writes=[r_gpost], key="c_gpost")
    DMA(gpost2[:], gpost2_d, writes=[r_gpost2], key="c_gpost2")
    OP("pool", "memset", [], [r_ones], ap=ones_bf[:], constant=1.0)
    WP.ensure(u_wout[-1])

    r_tmpc = Res("tmpc")
    lam = vecs[:, 70:80]
    xx, dd, zz, z2, pp = (tmpc[:, 0:10], tmpc[:, 10:20], tmpc[:, 20:30], tmpc[:, 30:40], tmpc[:, 40:50])
    TC = [r_tmpc]
    ACTV(xx, lam, AF.Exp, [r_vecs], TC, scale=-1.0)
    OP("dve", "tensor_scalar", TC, TC, out=dd, in0=xx, scalar1=2.0, scalar2=None, op0=ALU.add)
    OP("dve", "reciprocal", TC, TC, out=dd, in_=dd)
    OP("dve", "tensor_tensor", TC, TC, out=zz, in0=xx, in1=dd, op=ALU.mult)
    OP("dve", "tensor_tensor", TC, TC, out=z2, in0=zz, in1=zz, op=ALU.mult)
    OP("dve", "tensor_scalar", TC, TC, out=pp, in0=z2, scalar1=1.0 / 9, scalar2=1.0 / 7, op0=ALU.mult, op1=ALU.add)
    for cst in (1.0 / 5, 1.0 / 3, 1.0):
        OP("dve", "tensor_tensor", TC, TC, out=pp, in0=pp, in1=z2, op=ALU.mult)
        OP("dve", "tensor_scalar", TC, TC, out=pp, in0=pp, scalar1=cst, scalar2=None, op0=ALU.add)
    OP("dve", "tensor_tensor", TC, TC, out=pp, in0=pp, in1=zz, op=ALU.mult)
    OP("dve", "tensor_scalar", TC, [r_cf], out=cf[:, 0:10], in0=pp, scalar1=-16.0, scalar2=None, op0=ALU.mult)
    OP("dve", "tensor_scalar", TC + [r_cf], [r_cf], out=cf[:, 10:20], in0=pp, scalar1=-32.0, scalar2=None, op0=ALU.mult)

    r_rb, r_oh, r_ext, r_jm, r_sc = Res("rb"), Res("oh"), Res("ext"), Res("jm"), Res("sc")
    r_H = [Res("H0"), Res("H1")]
    DMA(rb[:], relb_d, writes=[r_rb], key="c_rb")
    DMA(oh[:], oh_d, writes=[r_oh], key="c_oh")
    DMA(jm[:], jmat_d, writes=[r_jm], key="c_jm")
    OP("pool", "memset", [], [r_ext], ap=ext[:], constant=NEG)
    for g in range(3):
        b = nextF()
        OP("pe", "matmul", [r_rb, r_oh], [bankF[b]], out=psF[0:12, b, 0:129], lhsT=rb[:], rhs=oh[:, g * 129:(g + 1) * 129], start=True, stop=True)
        OP("dve", "tensor_copy", [bankF[b]], [r_ext], out=ext[:, g, 127:256], in_=psF[0:12, b, 0:129])
    for g in range(3):
        DMA(sc_d[4 * g:4 * g + 4, :], ext[4 * g:4 * g + 4, g, :], reads=[r_ext], writes=[r_sc], key="c_sc")
    for gh in range(12):
        hb = gh % 2
        src = bass.AP(sc_d.tensor, gh * 383, [[1, 128], [1, 256]])
        DMA(Hb[hb][:], src, reads=[r_sc], writes=[r_H[hb]], key="c_H%d" % hb)
        b = nextF()
        OP("pe", "matmul", [r_jm, r_H[hb]], [bankF[b]], out=psF[:, b, 0:256], lhsT=jm[:], rhs=Hb[hb][:], start=True, stop=True)
        ACTV(Et[:, gh, :, :], psF[:, b, 0:256].rearrange("p (a b) -> p a b", b=128), AF.Exp, [bankF[b]], [r_E])
    if debug:
        r_dE = Res("dE")
        OP("dve", "tensor_copy", [r_E], [r_dE], out=dE[:], in_=Et[:].rearrange("p a b c -> p (a b c)"))
        dbg_toks.append(DMA(dbg["d_E"], dE[:], reads=[r_dE], key="dbg"))
        dbg_toks.append(DMA(dbg["d_cfac"][:, 0:16], cf[:, 0:16], reads=[r_cf], key="dbg"))

    out_toks = []
    base_dead = [r_tmpc, r_rb, r_oh, r_ext, r_jm, r_H[0], r_H[1]] + ([r_dE] if debug else [])
    prev_hn2T = None
    evi = [0]

    def evac_copy(dst, src, reads, writes, scale=None):
        evi[0] += 1
        if evi[0] % 2:
            if scale is not None:
                ACTV(dst, src, AF.Copy, reads, writes, scale=scale)
            else:
                ACTV(dst, src, AF.Copy, reads, writes)
        else:
            if scale is not None:
                OP("dve", "tensor_scalar", reads, writes, out=dst, in0=src, scalar1=scale, scalar2=None, op0=ALU.mult)
            else:
                OP("dve", "tensor_copy", reads, writes, out=dst, in_=src)

    def norm_stats(src_tile, r_src, hb, r_hb, jk, r_jk, r_st, scol):
        ss = stats[:, scol:scol + 1]
        sq = stats[:, scol + 1:scol + 2]
        rs = stats[:, scol + 2:scol + 3]
        ACTV(jk, src_tile, AF.Square, [r_src], [r_jk, r_st], accum_out=ss)
        ACTV(sq, ss, AF.Sqrt, [r_st], [r_st], scale=1.0 / D, bias=EPS)
        OP("dve", "reciprocal", [r_st], [r_st], out=rs, in_=sq)
        OP("dve", "tensor_scalar", [r_src, r_st], [r_hb], out=hb, in0=src_tile, scalar1=rs, scalar2=None, op0=ALU.mult)

    def transpose_to_T(hb, r_hb, gain, dstT, r_dst, tt):
        tb = nextT()
        for kc in range(8):
            OP("pe", "transpose", [r_hb, r_ident], [bankT[tb]], signal=(kc == 7),
               out=psT[:, tb, kc * 128:(kc + 1) * 128], in_=hb[:, kc * 128:(kc + 1) * 128], identity=ident_bf[:])
        OP("dve", "tensor_tensor", [bankT[tb], r_vecs], [r_dst], out=dstT[:, :, tt * 128:(tt + 1) * 128],
           in0=psT[:, tb, :].rearrange("p (a b) -> p a b", b=128), in1=bcast_free(gain, 128), op=ALU.mult)

    def proj_fm(wtile, nk, rhs_t, rhs_res, dst_fn, w_res, tgs=range(4)):
        for tg in tgs:
            b = nextF()
            for kc in range(nk):
                OP("pe", "matmul", [w_res] + rhs_res(tg), [bankF[b]], signal=(kc == nk - 1),
                   out=psF[:, b, :], lhsT=wtile[:, kc, :], rhs=rhs_t[:, kc, tg * 512:(tg + 1) * 512], start=(kc == 0), stop=(kc == nk - 1))
            dst_fn(tg, b)

    for s in range(nseq):
        us = useq[s]
        row0 = s * SEQ
        r_hnT = [Res("hnT%d" % t, after=base_dead + ([prev_hn2T[t]] if prev_hn2T else [])) for t in range(16)]
        r_yrnn = [Res("yrnn%d" % c, after=base_dead) for c in range(NRC)]
        r_oatt = [Res("oatt%d" % h, after=base_dead) for h in range(NH)]
        r_merged = [Res("merged%d" % m, after=base_dead) for m in range(8)]
        hn_res = lambda tg: r_hnT[tg * 4:(tg + 1) * 4]

        S.label = 'A'
        r_xt = [Res("xt%d" % i, after=base_dead) for i in range(2)]
        r_hnb = [Res("hnb%d" % i, after=base_dead) for i in range(2)]
        r_junkA = Res("junkA", after=base_dead)
        r_stA = [Res("statsA0"), Res("statsA1")]
        for tt in range(17):
            if tt < 16:
                bi = tt % 2
                DMA(xt[bi][:], x_d[row0 + tt * 128:row0 + (tt + 1) * 128, :], writes=[r_xt[bi]], key="xt%d" % bi)
                norm_stats(xt[bi][:], r_xt[bi], hnb[bi][:], r_hnb[bi], junkA[:], r_junkA, r_stA[bi], 4 * bi)
            if tt >= 1:
                t1 = tt - 1
                transpose_to_T(hnb[t1 % 2][:], r_hnb[t1 % 2], gpre, hnT, r_hnT[t1], t1)
        phaseA_res = r_xt + r_hnb + [r_junkA]
        if stop_after == "A":
            break

        dB = phaseA_res + base_dead
        r_q = [Res("q%d" % g, after=dB) for g in range(3)]
        r_k = Res("k", after=dB)
        r_v = Res("v", after=dB)
        r_Vb = [Res("Vb%d" % g, after=dB) for g in range(3)]
        r_acc = Res("acc", after=dB)
        r_Pf = [Res("Pf%d" % i, after=dB) for i in range(2)]
        r_Pb = [Res("Pb%d" % i, after=dB) for i in range(2)]
        r_rcp = Res("rcp", after=dB)
        NS_ = 3
        r_S = [bankF[3], bankF[4], bankF[5]]
        r_O = [bankF[6], bankF[7]]
        rot["modF"] = 3
        Sap = [psF[:, 3 + i, 0:256] for i in range(3)]
        Oap = [psF[:, 6 + i, 0:256] for i in range(2)]
        for h in range(NH):
            S.label = 'Bp'
            plan = [(0, 0, qT[0], r_q[0], SM_SCALE), (0, 1, qT[1], r_q[1], SM_SCALE), (1, 0, qT[2], r_q[2], SM_SCALE), (1, 1, kT, r_k, None), (2, 0, vT, r_v, None)]
            for (ui, sub, dstt, dres, scl) in plan:
                wt_, wres = WP.get(us["B"][h][ui])
                wt = wt_[:, sub * 1024:(sub + 1) * 1024].rearrange("p (a b) -> p a b", b=128)
                proj_fm(wt, 8, hnT, hn_res,
                        lambda tg, b, dstt=dstt, dres=dres, scl=scl: evac_copy(dstt[:, tg * 512:(tg + 1) * 512], psF[:, b, :], [bankF[b]], [dres], scale=scl),
                        wres)
            S.label = 'Bv'
            for g, (win, r) in enumerate(DIL):
                nb = SEQ // r // 128
                for half in range(2):
                    tb = nextT()
                    for jj in range(8):
                        blk = half * 8 + jj
                        c, n = blk // nb, blk % nb
                        t0 = c + r * 128 * n
                        OP("pe", "transpose", [r_v, r_ident], [bankT[tb]], signal=(jj == 7),
                           out=psT[:, tb, jj * 128:(jj + 1) * 128], in_=vT[:, t0:t0 + 127 * r + 1:r], identity=ident_bf[:])
                    evac_copy(Vb[g][:, half * 8:(half + 1) * 8, :], psT[:, tb, :].rearrange("p (a b) -> p a b", b=128), [bankT[tb]], [r_Vb[g]])
            S.label = 'Ba'
            blocks = []
            for g, (win, r) in enumerate(DIL):
                nb = SEQ // r // 128
                for blk in range(16):
                    c, n = blk // nb, blk % nb
                    blocks.append((g, r, blk, c + r * 128 * n, 2 if n > 0 else 1))
            NBLK = len(blocks)

            def st_qk(i):
                g, r, blk, t0, nkb = blocks[i]
                sp_ = Sap[i % NS_]
                qsl = qT[g][:, t0:t0 + 127 * r + 1:r]
                for kb in range(nkb):
                    tk = t0 - kb * r * 128
                    OP("pe", "matmul", [r_k, r_q[g]], [r_S[i % NS_]], signal=(kb == nkb - 1),
                       out=sp_[:, kb * 128:(kb + 1) * 128], lhsT=kT[:, tk:tk + 127 * r + 1:r], rhs=qsl, start=True, stop=True)

            def st_em(i):
                g, r, blk, t0, nkb = blocks[i]
                w = nkb * 128
                pi = i % 2
                Eflat = Et[:, g * 4 + h, :, :].rearrange("p a b -> p (a b)")
                ACTV(Pf[pi][:, 0:w], Sap[i % NS_][:, 0:w], AF.Exp, [r_S[i % NS_]], [r_Pf[pi]])
                OP("pool", "tensor_tensor", [r_Pf[pi], r_E], [r_Pb[pi]], out=Pb[pi][:, 0:w], in0=Pf[pi][:, 0:w], in1=Eflat[:, 0:w], op=ALU.mult)

            def st_pv(i):
                g, r, blk, t0, nkb = blocks[i]
                pi = i % 2
                op_ = Oap[i % 2]
                for part in range(2):
                    for kb in range(nkb):
                        lhs = Vb[g][:, blk - kb, :] if part == 0 else ones_bf[:]
                        OP("pe", "matmul", [r_Vb[g], r_ones, r_Pb[pi]], [r_O[i % 2]], signal=(part == 1 and kb == nkb - 1),
                           out=op_[:, part * 128:(part + 1) * 128], lhsT=lhs, rhs=Pb[pi][:, kb * 128:(kb + 1) * 128],
                           start=(kb == 0), stop=(kb == nkb - 1))
                osl = acc[:, :, t0:t0 + 127 * r + 1:r]
                psl = op_.rearrange("p (a b) -> p a b", b=128)
                if g == 0:
                    OP("dve", "tensor_copy", [r_O[i % 2]], [r_acc], out=osl, in_=psl)
                else:
                    OP("dve", "tensor_tensor", [r_O[i % 2], r_acc], [r_acc], out=osl, in0=osl, in1=psl, op=ALU.add)

            for step in range(NBLK + 2):
                if step < NBLK:
                    st_qk(step)
                if 0 <= step - 1 < NBLK:
                    st_em(step - 1)
                if 0 <= step - 2 < NBLK:
                    st_pv(step - 2)
            OP("dve", "reciprocal", [r_acc], [r_rcp], out=rcp[:], in_=acc[:, 1, :])
            OP("dve", "tensor_tensor", [r_acc, r_rcp], [r_oatt[h]], out=oatt[:, h, :], in0=acc[:, 0, :], in1=rcp[:], op=ALU.mult)
        rot["modF"] = 6
        phaseB_res = r_q + [r_k, r_v] + r_Vb + [r_acc] + r_Pf + r_Pb + [r_rcp]
        if stop_after == "B":
            break

        S.label = 'C'
        dC = phaseB_res + dB
        r_xr = [Res("xr%d" % i, after=dC) for i in range(2)]
        r_xc = [Res("xc%d" % i, after=dC) for i in range(2)]
        r_xcb = [Res("xcb%d" % i, after=dC) for i in range(2)]
        r_rr = [Res("rr%d" % i, after=dC) for i in range(2)]
        r_ii = [Res("ii%d" % i, after=dC) for i in range(2)]
        r_hcar = Res("hcar")
        OP("pool", "memset", [], [r_xr[0]], ap=xr_h[0][:, 0:4], constant=0.0)
        OP("pool", "memset", [], [r_xr[1]], ap=xr_h[1][:, 0:4], constant=0.0)
        NU = 2 * NRC
        cw_ = {}

        def c_s0(u):
            c, hf = u // 2, u % 2
            wt_, wres = WP.get(us["C"][c])
            cw_[u] = (wt_, wres)
            wt = wt_[:, 0:1024].rearrange("p (a b) -> p a b", b=128)
            if hf == 1:
                OP("pool", "tensor_copy", [r_xr[0]], [r_xr[1]], out=xr_h[1][:, 1:4], in_=xr_h[0][:, HS + 1:HS + 4])
            proj_fm(wt, 8, hnT, hn_res,
                    lambda tg, b: ACTV(xr_h[hf][:, 4 + (tg % 2) * 512:4 + (tg % 2 + 1) * 512], psF[:, b, :], AF.Copy, [bankF[b]], [r_xr[hf]]),
                    wres, tgs=(2 * hf, 2 * hf + 1))

        def c_s1(u):
            c, hf = u // 2, u % 2
            cw = [vecs[:, c * 4 + k:c * 4 + k + 1] for k in range(4)]
            OP("dve", "tensor_scalar", [r_xr[hf], r_vecs], [r_xc[hf]], out=xc_h[hf][:], in0=xr_h[hf][:, 4:4 + HS], scalar1=cw[3], scalar2=vecs[:, 40 + c:41 + c], op0=ALU.mult, op1=ALU.add)
            for k in range(3):
                OP("dve", "scalar_tensor_tensor", [r_xr[hf], r_vecs, r_xc[hf]], [r_xc[hf]], out=xc_h[hf][:], in0=xr_h[hf][:, 1 + k:1 + k + HS], scalar=cw[k], in1=xc_h[hf][:], op0=ALU.mult, op1=ALU.add)
            ACTV(xcb_h[hf][:], xc_h[hf][:], AF.Copy, [r_xc[hf]], [r_xcb[hf]])

        def c_s2(u):
            c, hf = u // 2, u % 2
            wt_, wres = cw_[u]
            wa = wt_[:, 1024:1152]
            wx = wt_[:, 1152:1280]
            for t2 in range(2):
                ba, bx = nextF(), nextF()
                sl = slice(t2 * 512, (t2 + 1) * 512)
                OP("pe", "matmul", [wres, r_xcb[hf]], [bankF[ba]], out=psF[:, ba, :], lhsT=wa, rhs=xcb_h[hf][:, sl], start=True, stop=True)
                OP("pe", "matmul", [wres, r_xcb[hf]], [bankF[bx]], out=psF[:, bx, :], lhsT=wx, rhs=xcb_h[hf][:, sl], start=True, stop=True)
                ACTV(rr_h[hf][:, sl], psF[:, ba, :], AF.Sigmoid, [bankF[ba], r_vecs], [r_rr[hf]], bias=vecs[:, 50 + c:51 + c])
                ACTV(ii_h[hf][:, sl], psF[:, bx, :], AF.Sigmoid, [bankF[bx], r_vecs], [r_ii[hf]], bias=vecs[:, 60 + c:61 + c])

        def c_s3(u):
            c, hf = u // 2, u % 2
            sq_ap = xr_h[hf][:, 4:4 + HS]
            ACTV(sq_ap, rr_h[hf][:], AF.Exp, [r_rr[hf], r_cf], [r_xr[hf]], scale=cf[:, 10 + c:11 + c])
            ACTV(sq_ap, sq_ap, AF.Sqrt, [r_xr[hf]], [r_xr[hf]], scale=-1.0, bias=1.0)
            ACTV(rr_h[hf][:], rr_h[hf][:], AF.Exp, [r_rr[hf], r_cf], [r_rr[hf]], scale=cf[:, c:c + 1])
            OP("pool", "tensor_tensor", [r_ii[hf], r_xc[hf]], [r_ii[hf]], out=ii_h[hf][:], in0=ii_h[hf][:], in1=xc_h[hf][:], op=ALU.mult)
            OP("dve", "tensor_tensor", [r_ii[hf], r_xr[hf]], [r_ii[hf]], out=ii_h[hf][:], in0=ii_h[hf][:], in1=sq_ap, op=ALU.mult)

        def c_s4(u):
            c, hf = u // 2, u % 2
            init = 0.0 if hf == 0 else hcar[:, 0:1]
            rds = [r_rr[hf], r_ii[hf]] + ([r_hcar] if hf == 1 else [])
            OP("dve", "tensor_tensor_scan", rds, [r_xc[hf]], out=xc_h[hf][:], data0=rr_h[hf][:], data1=ii_h[hf][:], initial=init, op0=ALU.mult, op1=ALU.add)
            if hf == 0:
                OP("pool", "tensor_copy", [r_xc[0]], [r_hcar], out=hcar[:, 0:1], in_=xc_h[0][:, HS - 1:HS])
            OP("pool", "tensor_copy", [r_xc[hf]], [r_yrnn[c]], out=yrnn[:, c, hf * HS:(hf + 1) * HS], in_=xc_h[hf][:])

        for step in range(NU + 2):
            if step < NU:
                c_s0(step)
            if 0 <= step - 2 < NU:
                c_s4(step - 2)
            if 0 <= step - 1 < NU:
                c_s2(step - 1)
                c_s3(step - 1)
            if step < NU:
                c_s1(step)
        phaseC_res = r_xr + r_xc + r_xcb + r_rr + r_ii

        if debug and s == 0:
            r_dbc = Res("dbufC", after=phaseC_res)
            for nm, t, rl in [("d_hnT", hnT, r_hnT), ("d_yrnn", yrnn, r_yrnn), ("d_oatt", oatt, r_oatt)]:
                for cc in range(t.shape[1]):
                    OP("dve", "tensor_copy", list(rl) + [r_dbc], [r_dbc], out=dbufC[:], in_=t[:, cc, :])
                    dbg_toks.append(DMA(dbg[nm][:, cc * SEQ:(cc + 1) * SEQ], dbufC[:], reads=[r_dbc], key="dbg"))
            phaseC_res = phaseC_res + [r_dbc]
        if stop_after == "C":
            break

        S.label = 'D1'
        dD = phaseC_res + dC
        r_tm = [[Res("tm%d_%d" % (a, i), after=dD) for i in range(2)] for a in range(2)]
        it = 0
        for m in range(8):
            wA, rA = WP.get(us["D1"][m][0])
            wB, rB = WP.get(us["D1"][m][1])
            wAv = wA[:, 0:2048].rearrange("p (a b) -> p a b", b=128)
            wBv = wB[:, 0:1792].rearrange("p (a b) -> p a b", b=128)
            for tg in range(4):
                sl = slice(tg * 512, (tg + 1) * 512)
                bgr, bga, bbr, bba = nextF(), nextF(), nextF(), nextF()
                for (b, wv, wr_, k0, nk, rhs_t, rres) in ((bgr, wAv, rA, 0, 8, hnT, hn_res(tg)), (bga, wAv, rA, 8, 8, hnT, hn_res(tg)),
                                                         (bbr, wBv, rB, 0, 10, yrnn, r_yrnn), (bba, wBv, rB, 10, 4, oatt, r_oatt)):
                    for kc in range(nk):
                        OP("pe", "matmul", [wr_] + list(rres), [bankF[b]], signal=(kc == nk - 1),
                           out=psF[:, b, :], lhsT=wv[:, k0 + kc, :], rhs=rhs_t[:, kc, sl], start=(kc == 0), stop=(kc == nk - 1))
                a = it % 2
                it += 1
                ACTV(tm[a][0][:], psF[:, bgr, :], AF.Sigmoid, [bankF[bgr]], [r_tm[a][0]])
                ACTV(tm[a][1][:], psF[:, bga, :], AF.Sigmoid, [bankF[bga]], [r_tm[a][1]])
                OP("dve", "tensor_tensor", [r_tm[a][0], bankF[bbr]], [r_tm[a][0]], out=tm[a][0][:], in0=tm[a][0][:], in1=psF[:, bbr, :], op=ALU.mult)
                OP("dve", "tensor_tensor", [r_tm[a][1], bankF[bba]], [r_tm[a][1]], out=tm[a][1][:], in0=tm[a][1][:], in1=psF[:, bba, :], op=ALU.mult)
                OP("pool", "tensor_tensor", [r_tm[a][0], r_tm[a][1]], [r_merged[m]], out=merged[:, m, sl], in0=tm[a][0][:], in1=tm[a][1][:], op=ALU.add)
        tm_res = r_tm[0] + r_tm[1]
        if debug and s == 0:
            r_dbuf = Res("dbuf", after=list(r_yrnn) + list(r_oatt))
            for cc in range(8):
                OP("dve", "tensor_copy", list(r_merged) + [r_dbuf], [r_dbuf], out=dbuf[:], in_=merged[:, cc, :])
                dbg_toks.append(DMA(dbg["d_merged"][:, cc * SEQ:(cc + 1) * SEQ], dbuf[:], reads=[r_dbuf], key="dbg"))
            dD = dD + [r_dbuf]
        if stop_after == "D1":
            break

        S.label = 'D2'
        dD2 = tm_res + dD + list(r_yrnn) + list(r_oatt)
        r_wdn = r_wdn_all[s]
        for rw_ in r_wdn:
            for o in dD2:
                if o.w is not None:
                    rw_._addr(o.w)
                for t_ in o.r.values():
                    rw_._addr(t_)
        for ui_ in us["Wd"]:
            WP.units[ui_]["hold"] = False
        r_xt2 = [Res("xt2_%d" % i, after=dD2) for i in range(2)]
        r_ht = [Res("ht%d" % i, after=dD2) for i in range(2)]
        r_hn2b = [Res("hn2b%d" % i, after=dD2) for i in range(2)]
        r_junk2 = Res("junk2", after=dD2)
        r_st2 = Res("st2")
        r_st2b = [Res("st2b0"), Res("st2b1")]
        r_hn2T = [Res("hn2T%d" % t, after=[r_hnT[t]]) for t in range(16)]
        r_hrow = [Res("hrow%d" % t) for t in range(16)]

        def d2_t0(tt):
            if tt < 12:
                WP.ensure(us["Wd"][tt])
            hb = tt % 2
            DMA(xt2[hb][:], x_d[row0 + tt * 128:row0 + (tt + 1) * 128, :], writes=[r_xt2[hb]], key="xt2_%d" % hb)
            p = nextP()
            for half in range(2):
                b = 2 * p + half
                for kc in range(8):
                    OP("pe", "matmul", list(r_merged) + [r_wout], [bankF[b]], signal=(kc == 7),
                       out=psF[:, b, :], lhsT=merged[:, kc, tt * 128:(tt + 1) * 128], rhs=w_out_bf[:, kc, half * 512:(half + 1) * 512], start=(kc == 0), stop=(kc == 7))
            mix = psF[:, 2 * p:2 * p + 2, :].rearrange("p a b -> p (a b)")
            pb = [bankF[2 * p], bankF[2 * p + 1]]
            ss, sq, rs = stats[:, 16:17], stats[:, 17:18], stats[:, 18:19]
            ACTV(junk2[:], mix, AF.Square, pb, [r_junk2, r_st2], accum_out=ss)
            ACTV(sq, ss, AF.Sqrt, [r_st2], [r_st2], scale=1.0 / D, bias=EPS)
            OP("dve", "reciprocal", [r_st2], [r_st2], out=rs, in_=sq)
            OP("dve", "scalar_tensor_tensor", pb + [r_st2, r_gpost], [r_ht[hb]], out=ht[hb][:], in0=mix, scalar=rs, in1=gpost[:], op0=ALU.mult, op1=ALU.mult)
            OP("pool", "tensor_tensor", [r_ht[hb], r_xt2[hb]], [r_ht[hb]], out=ht[hb][:], in0=ht[hb][:], in1=xt2[hb][:], op=ALU.add)
            DMA(out_d[row0 + tt * 128:row0 + (tt + 1) * 128, :], ht[hb][:], reads=[r_ht[hb]], writes=[r_hrow[tt]], key="hst%d" % hb)
            norm_stats(ht[hb][:], r_ht[hb], hn2b[hb][:], r_hn2b[hb], junk2[:], r_junk2, r_st2b[hb], 20 + 4 * hb)

        def d2_t1(tt):
            hb = tt % 2
            transpose_to_T(hn2b[hb][:], r_hn2b[hb], gpre2, hnT, r_hn2T[tt], tt)

        for step in range(17):
            if step < 16:
                d2_t0(step)
            if step >= 1:
                d2_t1(step - 1)
        phaseD2_res = r_xt2 + r_ht + r_hn2b + [r_junk2]
        if stop_after == "D2":
            out_toks = [r.w for r in r_hrow]
            break

        dE_ = phaseD2_res + dD2 + list(r_merged)
        r_gt = [Res("gt%d" % i, after=dE_) for i in range(2)]
        r_gc = [Res("gc%d" % i, after=dE_) for i in range(2)]
        r_ge = [Res("ge%d" % i, after=dE_) for i in range(2)]
        r_ot = Res("ot", after=dE_)
        r_hbuf = Res("hbuf", after=dE_)
        r_junk3 = Res("junk3", after=dE_)
        r_carry = Res("carry")
        r_st3 = Res("st3")
        r_ffT = [Res("ffT%d" % c, after=dE_) for c in range(NFC)]
        hn2_res = lambda tg: r_hn2T[tg * 4:(tg + 1) * 4]
        for tg in range(4):
            S.label = 'E'
            sl = slice(tg * 512, (tg + 1) * 512)
            for c in range(NFC):
                bi = c % 2
                wt_, wres = WP.get(us["E"][tg][c])
                wv = wt_[:, 0:2048].rearrange("p (a b) -> p a b", b=128)
                bg, bu = nextF(), nextF()
                for (b, k0) in ((bg, 0), (bu, 8)):
                    for kc in range(8):
                        OP("pe", "matmul", [wres] + hn2_res(tg), [bankF[b]], signal=(kc == 7),
                           out=psF[:, b, :], lhsT=wv[:, k0 + kc, :], rhs=hnT[:, kc, sl], start=(kc == 0), stop=(kc == 7))
                if tg == 0:
                    OP("pool", "memset", [], [r_gt[bi]], ap=gt[bi][:, 0:4], constant=0.0)
                else:
                    OP("pool", "tensor_copy", [r_carry], [r_gt[bi]], out=gt[bi][:, 2:4], in_=carry[:, c, :])
                ACTV(gt[bi][:, 4:516], psF[:, bg, :], AF.Copy, [bankF[bg]], [r_gt[bi]])
                if tg < 3:
                    OP("pool", "tensor_copy", [r_gt[bi]], [r_carry], out=carry[:, c, :], in_=gt[bi][:, 514:516])
                fw = [vecs[:, 80 + c * 3 + k:80 + c * 3 + k + 1] for k in range(3)]
                OP("dve", "tensor_scalar", [r_gt[bi], r_vecs], [r_gc[bi]], out=gc[bi][:], in0=gt[bi][:, 4:516], scalar1=fw[2], scalar2=vecs[:, 152 + c:153 + c], op0=ALU.mult, op1=ALU.add)
                for k in range(2):
                    OP("dve", "scalar_tensor_tensor", [r_gt[bi], r_vecs, r_gc[bi]], [r_gc[bi]], out=gc[bi][:], in0=gt[bi][:, 2 + k:2 + k + 512], scalar=fw[k], in1=gc[bi][:], op0=ALU.mult, op1=ALU.add)
                ACTV(ge[bi][:], gc[bi][:], AF.Gelu_apprx_tanh, [r_gc[bi]], [r_ge[bi]])
                OP("dve", "tensor_tensor", [r_ge[bi], bankF[bu]], [r_ffT[c]], out=ffT[:, c, :], in0=ge[bi][:], in1=psF[:, bu, :], op=ALU.mult)
            S.label = 'Ed'
            for t4 in range(4):
                tt = tg * 4 + t4
                DMA(hbuf[:], out_d[row0 + tt * 128:row0 + (tt + 1) * 128, :], reads=[r_hrow[tt]], writes=[r_hbuf], key="hbuf")
                p = nextP()
                for half in range(2):
                    b = 2 * p + half
                    for kc in range(NFC):
                        OP("pe", "matmul", list(r_ffT) + list(r_wdn), [bankF[b]], signal=(kc == NFC - 1),
                           out=psF[:, b, :], lhsT=ffT[:, kc, t4 * 128:(t4 + 1) * 128], rhs=wdn[:, kc, half * 512:(half + 1) * 512], start=(kc == 0), stop=(kc == NFC - 1))
                ff = psF[:, 2 * p:2 * p + 2, :].rearrange("p a b -> p (a b)")
                pb = [bankF[2 * p], bankF[2 * p + 1]]
                ss, sq, rs = stats[:, 32:33], stats[:, 33:34], stats[:, 34:35]
                ACTV(junk3[:], ff, AF.Square, pb, [r_junk3, r_st3], accum_out=ss)
                ACTV(sq, ss, AF.Sqrt, [r_st3], [r_st3], scale=1.0 / D, bias=EPS)
                OP("dve", "reciprocal", [r_st3], [r_st3], out=rs, in_=sq)
                OP("dve", "scalar_tensor_tensor", pb + [r_st3, r_gpost2], [r_ot], out=ot[:], in0=ff, scalar=rs, in1=gpost2[:], op0=ALU.mult, op1=ALU.mult)
                OP("pool", "tensor_tensor", [r_ot, r_hbuf], [r_ot], out=ot[:], in0=ot[:], in1=hbuf[:], op=ALU.add)
                out_toks.append(DMA(out_d[row0 + tt * 128:row0 + (tt + 1) * 128, :], ot[:], reads=[r_ot], writes=[r_hrow[tt]], key="ost"))
        base_dead = r_gt + r_gc + r_ge + [r_ot, r_hbuf, r_junk3] + list(r_ffT) + list(r_wdn)
        prev_hn2T = r_hn2T

    S.wait_tokens("sp", out_toks + dbg_toks)
    run_block(nc, S)
    SCHED_STATS.clear()
    SCHED_STATS.update({e: len(S.ops[e]) for e in ENGS})
    SCHED_STATS['labels'] = S.labels
    return nc


def host_consts(inp):
    f = lambda a: np.ascontiguousarray(np.asarray(a, dtype=np.float32))
    vecs = np.zeros((128, NV), np.float32)
    crw = f(inp["conv_rnn_w"])[0]
    vecs[:, 0:40] = crw.reshape(4, NRC, 128).transpose(2, 1, 0).reshape(128, 40)
    vecs[:, 40:50] = f(inp["conv_rnn_b"])[0].reshape(NRC, 128).T
    vecs[:, 50:60] = f(inp["b_rg_a"])[0].reshape(NRC, 128).T
    vecs[:, 60:70] = f(inp["b_rg_x"])[0].reshape(NRC, 128).T
    vecs[:, 70:80] = f(inp["lru_lambda"])[0].reshape(NRC, 128).T
    cfw = f(inp["conv_ffn_w"])[0]
    vecs[:, 80:152] = cfw.reshape(3, NFC, 128).transpose(2, 1, 0).reshape(128, 72)
    vecs[:, 152:176] = f(inp["conv_ffn_b"])[0].reshape(NFC, 128).T
    vecs[:, 176:184] = f(inp["norm_mix_pre"])[0].reshape(8, 128).T
    vecs[:, 184:192] = f(inp["norm_ffn_pre"])[0].reshape(8, 128).T
    oh = np.zeros((32, 3 * 129), np.float32)
    for g, (win, r) in enumerate(DIL):
        bk = _t5_bucket(np.arange(129) * r)
        oh[bk, g * 129 + np.arange(129)] = 1.0
    shared = {
        "w_in": f(inp["w_in"])[0],
        "w_rg_a": f(inp["w_rg_a"])[0].reshape(NRC * 128, 128),
        "w_rg_x": f(inp["w_rg_x"])[0].reshape(NRC * 128, 128),
        "w_branch_rnn": f(inp["w_branch_rnn"])[0],
        "w_branch_att": f(inp["w_branch_att"])[0],
        "w_out": f(inp["w_out"])[0],
        "w_ffn_gate": f(inp["w_ffn_gate"])[0],
        "w_ffn_up": f(inp["w_ffn_up"])[0],
        "w_ffn_down": f(inp["w_ffn_down"])[0],
        "rel_bias": f(inp["rel_bias"]),
        "oh": oh,
        "vecs": vecs,
        "gpost": np.ascontiguousarray(np.broadcast_to(f(inp["norm_mix_post"])[0], (128, D))),
        "gpost2": np.ascontiguousarray(np.broadcast_to(f(inp["norm_ffn_post"])[0], (128, D))),
        "ident": np.eye(128, dtype=np.float32),
        "jmat": np.ascontiguousarray(np.eye(128, dtype=np.float32)[::-1]),
    }
    return shared


_NC_CACHE = {}


def kernel(**inputs):
    x = np.asarray(inputs["x"], dtype=np.float32)
    B = x.shape[0]
    nseq = B // NCORES
    shared = host_consts(inputs)
    if nseq not in _NC_CACHE:
        _NC_CACHE[nseq] = build(nseq)
    nc = _NC_CACHE[nseq]
    in_maps = []
    for c in range(NCORES):
        m = dict(shared)
        m["x"] = np.ascontiguousarray(x[c * nseq:(c + 1) * nseq].reshape(nseq * SEQ, D))
        in_maps.append(m)
    res = run_bass_kernel_spmd(nc, in_maps, core_ids=list(range(NCORES)))
    outs = [np.asarray(r["out"]).reshape(nseq, SEQ, D) for r in res.results]
    return np.concatenate(outs, axis=0).astype(np.float32)
```

```python
import contextlib
import math
import numpy as np
import concourse.bass as bass
import concourse.mybir as mybir
from concourse.bass_utils import run_bass_kernel_spmd

F32 = mybir.dt.float32
BF16 = mybir.dt.bfloat16
AF = mybir.ActivationFunctionType
ALU = mybir.AluOpType

ENGS = ["pe", "act", "dve", "pool", "sp"]
EPOCH_LIMIT = 12000


class Res:
    __slots__ = ("name", "w", "r", "const")

    def __init__(self, name, after=(), const=False):
        self.name = name
        self.w = None
        self.r = {}
        self.const = const
        for o in after:
            if o.w is not None:
                self._addr(o.w)
            for t in o.r.values():
                self._addr(t)

    def _addr(self, t):
        k = t[1]
        o = self.r.get(k)
        if o is None or o[2] < t[2]:
            self.r[k] = t


class Sched:
    def __init__(self):
        self.ops = {e: [] for e in ENGS}
        self.count = {e: 0 for e in ENGS}
        self.epoch = {e: 0 for e in ENGS}
        self.known = {e: {} for e in ENGS}
        self.prev_epochs = {e: {} for e in ENGS}
        self.dma_cnt = {}
        self.pend_r = {e: [] for e in ENGS}
        self.pend_w = {e: [] for e in ENGS}
        self.semkeys = set()
        self.label = ''
        self.labels = {e: [] for e in ENGS}

    def _need(self, eng, tok, waits):
        key, val, snap = tok[1], tok[2], tok[3]
        kn = self.known[eng]
        if kn.get(key, 0) >= val:
            return
        waits.append((key, val))
        kn[key] = val
        for k, v in snap.items():
            if kn.get(k, 0) < v:
                kn[k] = v

    def op(self, eng, fn, reads=(), writes=(), dma=None, signal=True):
        waits = []
        isdma = dma is not None
        for r in reads:
            t = r.w
            if t is not None:
                if t[0] == eng and not isdma and eng == "pe":
                    continue
                self._need(eng, t, waits)
        strict = isdma or eng == "pool"
        for w in writes:
            t = w.w
            if t is not None and (t[0] != eng or strict):
                self._need(eng, t, waits)
            for t in w.r.values():
                if t[0] != eng or strict:
                    self._need(eng, t, waits)
        if isdma:
            key = ("dma", dma)
            ndma = len(fn) if isinstance(fn, list) else 1
            self.dma_cnt[key] = self.dma_cnt.get(key, 0) + 16 * ndma
            val = self.dma_cnt[key]
            tok = ("dma", key, val, dict(self.known[eng]))
            inc = (key, 16)
            self.semkeys.add(key)
        elif signal:
            if self.count[eng] >= EPOCH_LIMIT:
                self.prev_epochs[eng][(eng, self.epoch[eng])] = self.count[eng]
                self.epoch[eng] += 1
                self.count[eng] = 0
            self.count[eng] += 1
            key = (eng, self.epoch[eng])
            val = self.count[eng]
            snap = dict(self.known[eng])
            snap.update(self.prev_epochs[eng])
            tok = (eng, key, val, snap)
            inc = (key, 1)
            self.semkeys.add(key)
        else:
            tok = None
            inc = None
        if tok is None:
            self.pend_r[eng].extend(r for r in reads if not r.const)
            self.pend_w[eng].extend(writes)
        else:
            if not isdma:
                for r in self.pend_r[eng]:
                    r._addr(tok)
                for w in self.pend_w[eng]:
                    w.w = tok
                    w.r = {}
                self.pend_r[eng] = []
                self.pend_w[eng] = []
            for r in reads:
                if not r.const:
                    r._addr(tok)
            for w in writes:
                w.w = tok
                w.r = {}
        self.ops[eng].append((waits, fn, inc))
        self.labels[eng].append(self.label)
        return tok

    def wait_tokens(self, eng, toks):
        waits = []
        for t in toks:
            if t is not None:
                self._need(eng, t, waits)
        if waits:
            self.ops[eng].append((waits, None, None))


def _replay(sched, eng, e, sems):
    for waits, fn, inc in sched.ops[eng]:
        for key, val in waits:
            e.wait_ge(sems[key], val)
        if fn is None:
            continue
        if isinstance(fn, list):
            for meth, kw in fn:
                getattr(e, meth)(**kw).then_inc(sems[inc[0]], 16)
            continue
        meth, kw = fn
        ins = getattr(e, meth)(**kw)
        if inc is not None:
            ins.then_inc(sems[inc[0]], inc[1])


def run_block(nc, sched):
    with contextlib.ExitStack() as st:
        sems = {}
        for i, key in enumerate(sorted(sched.semkeys, key=str)):
            sems[key] = st.enter_context(nc.semaphore("s%d" % i))
        block = st.enter_context(nc.Block())

        @block.tensor
        def _(e):
            _replay(sched, "pe", e, sems)

        @block.scalar
        def _(e):
            _replay(sched, "act", e, sems)

        @block.vector
        def _(e):
            _replay(sched, "dve", e, sems)

        @block.gpsimd
        def _(e):
            _replay(sched, "pool", e, sems)

        @block.sync
        def _(e):
            _replay(sched, "sp", e, sems)


D = 1024
SEQ = 2048
NCORES = 8
RNN_W = 1280
NRC = 10
HD = 128
NH = 4
DIL = ((128, 1), (512, 4), (2048, 16))
FFN = 3072
NFC = 24
EPS = 1e-6
IN_W = 5888
OFF_Q = 1280
OFF_K = OFF_Q + 1536
OFF_V = OFF_K + 512
OFF_GR = OFF_V + 512
OFF_GA = OFF_GR + 1024
NEG = -30000.0
NV = 192
SM_SCALE = HD ** -0.5
SCHED_STATS = {}


def _t5_bucket(dist):
    max_exact = 16
    d = np.maximum(dist, 1).astype(np.float32)
    large = max_exact + np.log(d / max_exact) / math.log(2048 / max_exact) * (32 - max_exact)
    large = np.minimum(large.astype(np.int32), 31)
    return np.where(dist < max_exact, dist, large).astype(np.int32)


def bcast_free(ap2d, n):
    return bass.AP(ap2d.tensor, ap2d.offset, [list(ap2d.ap[0]), list(ap2d.ap[1]), [0, n]])


def build(nseq, debug=False, stop_after=None):
    nc = bass.Bass("TRN2", target_bir_lowering=False)
    T = nseq * SEQ

    def dram(name, shape, kind="ExternalInput"):
        return nc.dram_tensor(name, list(shape), F32, kind=kind).ap()

    x_d = dram("x", [T, D])
    out_d = dram("out", [T, D], "ExternalOutput")
    w_in_d = dram("w_in", [D, IN_W])
    w_rga_d = dram("w_rg_a", [NRC * 128, 128])
    w_rgx_d = dram("w_rg_x", [NRC * 128, 128])
    w_br_d = dram("w_branch_rnn", [RNN_W, D])
    w_ba_d = dram("w_branch_att", [512, D])
    w_out_d = dram("w_out", [D, D])
    w_g_d = dram("w_ffn_gate", [D, FFN])
    w_u_d = dram("w_ffn_up", [D, FFN])
    w_d_d = dram("w_ffn_down", [FFN, D])
    relb_d = dram("rel_bias", [32, 12])
    oh_d = dram("oh", [32, 3 * 129])
    vecs_d = dram("vecs", [128, NV])
    gpost_d = dram("gpost", [128, D])
    gpost2_d = dram("gpost2", [128, D])
    ident_d = dram("ident", [128, 128])
    jmat_d = dram("jmat", [128, 128])
    sc_d = nc.dram_tensor("scratch_ext", [12, 383], F32).ap()
    wgu_d = nc.dram_tensor("scratch_wgu", [NFC, 128, 2048], BF16).ap()
    dbg = {}
    if debug:
        for nm, shp in [("d_hnT", [128, 8 * SEQ]), ("d_yrnn", [128, NRC * SEQ]), ("d_oatt", [128, 4 * SEQ]),
                        ("d_merged", [128, 8 * SEQ]), ("d_E", [128, 12 * 256]), ("d_cfac", [128, 16])]:
            dbg[nm] = dram(nm, shp, "ExternalOutput")

    w_in_v = w_in_d.rearrange("(kc p) n -> p kc n", p=128)
    w_g_v = w_g_d.rearrange("(kc p) n -> p kc n", p=128)
    w_u_v = w_u_d.rearrange("(kc p) n -> p kc n", p=128)
    w_br_v = w_br_d.rearrange("(kc p) n -> p kc n", p=128)
    w_ba_v = w_ba_d.rearrange("(kc p) n -> p kc n", p=128)
    w_out_v = w_out_d.rearrange("(kc p) n -> p kc n", p=128)
    w_d_v = w_d_d.rearrange("(kc p) n -> p kc n", p=128)
    w_rga_v = w_rga_d.rearrange("(c p) n -> p c n", p=128)
    w_rgx_v = w_rgx_d.rearrange("(c p) n -> p c n", p=128)

    BASE = 24576
    LIMIT = 229312
    cur = [BASE]

    def take(nb):
        o = cur[0]
        cur[0] += (nb + 63) // 64 * 64
        assert cur[0] <= LIMIT, "SBUF overflow %d" % cur[0]
        return o

    cnt = [0]

    def sbt(shape, dt, off):
        cnt[0] += 1
        return nc.alloc_sbuf_tensor_at("t%d" % cnt[0], list(shape), dt, offset=off)

    def nbytes(shape, dt):
        return int(np.prod(shape[1:])) * (4 if dt == F32 else 2)

    def new(shape, dt):
        return sbt(shape, dt, take(nbytes(shape, dt)))

    ident_bf = new([128, 128], BF16)
    ones_bf = new([128, 128], BF16)
    vecs = new([128, NV], F32)
    cf = new([128, 32], F32)
    hbias = new([128, 32], F32)
    Et = new([128, 12, 2, 128], BF16)
    gpost = new([128, D], F32)
    gpost2 = new([128, D], F32)
    w_out_bf = new([128, 8, D], BF16)
    stats = new([128, 64], F32)
    carry = new([128, NFC, 2], F32)
    hcar = new([128, 16], F32)
    stg = [new([128, 2048], F32) for _ in range(2)]
    wbf = [new([128, 2048], BF16) for _ in range(4)]
    R1_off = take(8 * SEQ * 2)
    R2_off = take(14 * SEQ * 2)
    R3_off = take(8 * SEQ * 2)
    W_off = cur[0]
    W_size = LIMIT - W_off
    R3_size = 8 * SEQ * 2

    hnT = sbt([128, 8, SEQ], BF16, R1_off)
    yrnn = sbt([128, NRC, SEQ], BF16, R2_off)
    oatt = sbt([128, 4, SEQ], BF16, R2_off + NRC * SEQ * 2)
    wdn = sbt([128, NFC, D], BF16, R2_off)
    R2_spare = R2_off + NFC * D * 2
    merged = sbt([128, 8, SEQ], BF16, R3_off)
    ffT = sbt([128, NFC, 512], BF16, R3_off)
    R3_spare = R3_off + NFC * 512 * 2

    class Region:
        def __init__(self, spans):
            self.spans = [list(sp) for sp in spans]
            self.i = 0
            self.o = self.spans[0][0]

        def new(self, shape, dt):
            nb = (nbytes(shape, dt) + 63) // 64 * 64
            while True:
                sp = self.spans[self.i]
                if self.o + nb <= sp[0] + sp[1]:
                    o = self.o
                    self.o += nb
                    return sbt(shape, dt, o)
                self.i += 1
                assert self.i < len(self.spans), "region overflow"
                self.o = self.spans[self.i][0]

    WR = Region([(W_off, W_size), (R3_off, R3_size)])
    tmpc = WR.new([128, 64], F32)
    rb = WR.new([32, 12], F32)
    oh = WR.new([32, 3 * 129], F32)
    ext = WR.new([12, 3, 383], F32)
    jm = WR.new([128, 128], F32)
    Hb = [WR.new([128, 256], F32) for _ in range(2)]
    dE = WR.new([128, 12 * 256], F32) if debug else None

    RA = Region([(R3_off, R3_size)])
    xt = [RA.new([128, D], F32) for _ in range(2)]
    hnb = [RA.new([128, D], BF16) for _ in range(2)]

    RB = Region([(R3_off, R3_size), (R2_off, NRC * SEQ * 2), (W_off, W_size)])
    qT = [RB.new([128, SEQ], BF16) for _ in range(3)]
    kT = RB.new([128, SEQ], BF16)
    vT = RB.new([128, SEQ], BF16)
    Vb = [RB.new([128, 16, 128], BF16) for _ in range(3)]
    acc = RB.new([128, 2, SEQ], F32)
    Pf = [RB.new([128, 256], F32) for _ in range(2)]
    Pb = [RB.new([128, 256], BF16) for _ in range(2)]
    rcp = RB.new([128, SEQ], F32)

    HS = SEQ // 2
    NSET = 3
    RC = Region([(R3_off, R3_size + W_size)])
    xr_h = [RC.new([128, HS + 4], F32) for _ in range(NSET)]
    xc_h = [RC.new([128, HS], F32) for _ in range(NSET)]
    rr_h = [RC.new([128, HS], F32) for _ in range(NSET)]
    xi_h = [RC.new([128, HS], BF16) for _ in range(NSET)]
    dbufC = None

    dbuf = sbt([128, SEQ], F32, R2_off)
    RD = Region([(W_off, W_size)])
    tm = [[RD.new([128, 512], F32) for _ in range(2)] for _ in range(2)]

    RD2 = Region([(W_off, W_size), (R2_spare, 8192)])
    ht = [RD2.new([128, D], F32) for _ in range(2)]
    xt2 = [RD2.new([128, D], F32) for _ in range(2)]
    hn2b = [RD2.new([128, D], BF16) for _ in range(2)]

    RE = Region([(W_off, W_size), (R3_spare, 8192), (R2_spare, 8192)])
    gt = [RE.new([128, 516], F32) for _ in range(2)]
    gc = [RE.new([128, 512], F32) for _ in range(2)]
    ge = [RE.new([128, 512], F32) for _ in range(2)]
    ot = RE.new([128, D], F32)
    hbuf = RE.new([128, D], F32)

    psF = nc.alloc_psum_tensor("psF", [128, 8, 512], F32)
    psT = psF[:, 6:8, :].bitcast(BF16)
    bankF = [Res("bankF%d" % i) for i in range(8)]
    bankT = [bankF[6], bankF[7]]
    rot = {"F": 0, "T": 0, "P": 0, "modF": 6}

    def nextF():
        b = rot["F"] % rot["modF"]
        rot["F"] = (b + 1) % rot["modF"]
        return b

    def nextT():
        b = rot["T"]
        rot["T"] = (b + 1) % 2
        return b

    def nextP():
        b = rot["P"]
        rot["P"] = (b + 1) % 3
        return b

    S = Sched()
    dbg_toks = []

    def OP(eng, meth, reads=(), writes=(), signal=True, **kw):
        return S.op(eng, (meth, kw), reads=reads, writes=writes, signal=signal)

    def DMA(out, in_, reads=(), writes=(), key=None):
        return S.op("sp", [("dma_start", dict(out=out, in_=in_))], reads=reads, writes=writes, dma=key)

    def ACTV(out, in_, func, reads, writes, **kw):
        return OP("act", "activation", reads, writes, out=out, in_=in_, func=func, **kw)

    CR = lambda n: Res(n, const=True)
    r_ident, r_ones, r_vecs, r_cf, r_E = CR("ident"), CR("ones"), CR("vecs"), CR("cf"), CR("E")
    r_gpost, r_gpost2, r_wout = CR("gpost"), CR("gpost2"), CR("wout")
    r_stg = [Res("stg0"), Res("stg1")]
    r_wbf = [Res("wbf%d" % i) for i in range(4)]

    class WPipe:
        LA = 2

        def __init__(self):
            self.units = []
            self.ns = 0
            self.ncast = 0
            self.nslot = 0
            self.nstg = 0

        def add(self, pieces, fixed=None, scale=None):
            n = sum(int(np.prod(sh)) for _, sh in pieces)
            assert n <= 2048
            self.units.append(dict(pieces=pieces, n=n, fixed=fixed, slot=None, scale=scale))
            return len(self.units) - 1

        def add_direct(self, src, res):
            self.units.append(dict(direct=(src, res), n=2048, fixed=None, slot=None))
            return len(self.units) - 1

        def _stage(self, j):
            u = self.units[j]
            if "direct" in u:
                src, res = u["direct"]
                slot = self.nslot % 4
                self.nslot += 1
                u["slot"] = slot
                S.op("sp", [("dma_start", dict(out=wbf[slot][:, 0:2048], in_=src))], reads=[res], writes=[r_wbf[slot]], dma="wdir%d" % slot)
                return
            si = self.nstg % 2
            self.nstg += 1
            u["si"] = si
            o = 0
            calls = []
            for src, shape in u["pieces"]:
                n = int(np.prod(shape))
                if len(shape) == 2:
                    dst = stg[si][:, o:o + n].rearrange("p (a b) -> p a b", b=shape[1])
                else:
                    dst = stg[si][:, o:o + n]
                calls.append(("dma_start", dict(out=dst, in_=src)))
                o += n
            S.op("sp", calls, writes=[r_stg[si]], dma="stg%d" % si)

        def _cast(self, j):
            u = self.units[j]
            if "direct" in u:
                return
            si = u["si"]
            n = u["n"]
            if u["fixed"] is not None:
                dst, dres = u["fixed"]
            else:
                slot = self.nslot % 4
                self.nslot += 1
                u["slot"] = slot
                dst, dres = wbf[slot][:, 0:n], r_wbf[slot]
            if u.get("scale") is not None:
                ACTV(dst, stg[si][:, 0:n], AF.Copy, [r_stg[si]], [dres], scale=u["scale"])
            else:
                ACTV(dst, stg[si][:, 0:n], AF.Copy, [r_stg[si]], [dres])

        def ensure(self, k):
            N = len(self.units)
            tc = min(k + 1 + self.LA, N)
            while True:
                if self.ns < N and self.ns < self.ncast + 2 and ("direct" not in self.units[self.ns] or self.ns < tc):
                    self._stage(self.ns)
                    self.ns += 1
                elif self.ncast < tc and not self.units[self.ncast].get("hold"):
                    self._cast(self.ncast)
                    self.ncast += 1
                else:
                    break

        def get(self, k):
            self.ensure(k)
            sl = self.units[k]["slot"]
            return wbf[sl], r_wbf[sl]

    WP = WPipe()
    u_ident = WP.add([(ident_d, [128])], fixed=(ident_bf[:], r_ident))
    u_wout = [WP.add([(w_out_v[:, :, q4 * 256:(q4 + 1) * 256], [8, 256])],
                     fixed=(w_out_bf[:, :, q4 * 256:(q4 + 1) * 256], r_wout)) for q4 in range(4)]

    u_gu = [WP.add([(w_g_v[:, :, c * 128:(c + 1) * 128], [8, 128]), (w_u_v[:, :, c * 128:(c + 1) * 128], [8, 128])]) for c in range(NFC)]
    r_wgu = [Res("wgu%d" % c) for c in range(NFC)]
    useq = []
    r_wdn_all = []
    for s in range(nseq):
        us = {}
        us["B"] = []
        for h in range(NH):
            cq = [OFF_Q + (j * 4 + h) * 128 for j in range(3)]
            ck, cv = OFF_K + h * 128, OFF_V + h * 128
            wsl = lambda c0: (w_in_v[:, :, c0:c0 + 128], [8, 128])
            us["B"].append([WP.add([wsl(cq[0]), wsl(cq[1])]), WP.add([wsl(cq[2]), wsl(ck)]), WP.add([wsl(cv)])])
        us["C"] = [WP.add([(w_in_v[:, :, c * 128:(c + 1) * 128], [8, 128]), (w_rga_v[:, c, :], [128]), (w_rgx_v[:, c, :], [128])]) for c in range(NRC)]
        us["D1"] = []
        for m in range(8):
            cs = slice(m * 128, (m + 1) * 128)
            a = WP.add([(w_in_v[:, :, OFF_GR + m * 128:OFF_GR + (m + 1) * 128], [8, 128]), (w_in_v[:, :, OFF_GA + m * 128:OFF_GA + (m + 1) * 128], [8, 128])])
            b = WP.add([(w_br_v[:, :, cs], [10, 128]), (w_ba_v[:, :, cs], [4, 128])])
            us["D1"].append((a, b))
        rw = [Res("wdn%d_%d" % (s, i)) for i in range(12)]
        r_wdn_all.append(rw)
        us["Wd"] = [WP.add([(w_d_v[:, 2 * t:2 * t + 2, :], [2, D])], fixed=(wdn[:, 2 * t:2 * t + 2, :].rearrange("p a b -> p (a b)"), rw[t])) for t in range(12)]
        for ui_ in us["Wd"]:
            WP.units[ui_]["hold"] = True
        us["E"] = [[WP.add_direct(wgu_d[c], r_wgu[c]) for c in range(NFC)] for tg in range(4)]
        useq.append(us)
    for q4, uidx in enumerate(u_wout):
        WP.units[uidx]["fixed"] = None
        WP.units[uidx]["wout_q4"] = q4

    _orig_cast = WP._cast

    def _cast2(j):
        u = WP.units[j]
        if "wout_q4" in u:
            q4 = u["wout_q4"]
            si = u["si"]
            ACTV(w_out_bf[:, :, q4 * 256:(q4 + 1) * 256], stg[si][:, 0:2048].rearrange("p (a b) -> p a b", b=256), AF.Copy, [r_stg[si]], [r_wout])
        else:
            _orig_cast(j)
    WP._cast = _cast2

    gpre = vecs[:, 176:184]
    gpre2 = vecs[:, 184:192]
    DMA(vecs[:], vecs_d, writes=[r_vecs], key="c_vecs")
    DMA(gpost[:], gpost_d, writes=[r_gpost], key="c_gpost")
    DMA(gpost2[:], gpost2_d, writes=[r_gpost2], key="c_gpost2")
    OP("pool", "memset", [], [r_ones], ap=ones_bf[:], constant=1.0)
    WP.ensure(u_wout[-1])
    for c in range(NFC):
        wt_, wres = WP.get(u_gu[c])
        S.op("sp", [("dma_start", dict(out=wgu_d[c], in_=wt_[:, 0:2048]))], reads=[wres], writes=[r_wgu[c]], dma="wgu_st%d" % WP.units[u_gu[c]]["slot"])

    r_tmpc = Res("tmpc")
    lam = vecs[:, 70:80]
    xx, dd, zz, z2, pp = (tmpc[:, 0:10], tmpc[:, 10:20], tmpc[:, 20:30], tmpc[:, 30:40], tmpc[:, 40:50])
    TC = [r_tmpc]
    ACTV(xx, lam, AF.Exp, [r_vecs], TC, scale=-1.0)
    OP("dve", "tensor_scalar", TC, TC, out=dd, in0=xx, scalar1=2.0, scalar2=None, op0=ALU.add)
    OP("dve", "reciprocal", TC, TC, out=dd, in_=dd)
    OP("dve", "tensor_tensor", TC, TC, out=zz, in0=xx, in1=dd, op=ALU.mult)
    OP("dve", "tensor_tensor", TC, TC, out=z2, in0=zz, in1=zz, op=ALU.mult)
    OP("dve", "tensor_scalar", TC, TC, out=pp, in0=z2, scalar1=1.0 / 9, scalar2=1.0 / 7, op0=ALU.mult, op1=ALU.add)
    for cst in (1.0 / 5, 1.0 / 3, 1.0):
        OP("dve", "tensor_tensor", TC, TC, out=pp, in0=pp, in1=z2, op=ALU.mult)
        OP("dve", "tensor_scalar", TC, TC, out=pp, in0=pp, scalar1=cst, scalar2=None, op0=ALU.add)
    OP("dve", "tensor_tensor", TC, TC, out=pp, in0=pp, in1=zz, op=ALU.mult)
    OP("dve", "tensor_scalar", TC, [r_cf], out=cf[:, 0:10], in0=pp, scalar1=-16.0, scalar2=None, op0=ALU.mult)
    OP("dve", "tensor_scalar", TC + [r_cf], [r_cf], out=cf[:, 10:20], in0=pp, scalar1=-32.0, scalar2=None, op0=ALU.mult)
    OP("dve", "tensor_scalar", TC + [r_cf], [r_cf], out=cf[:, 20:30], in0=pp, scalar1=-8.0, scalar2=None, op0=ALU.mult)
    OP("dve", "tensor_scalar", [r_vecs, r_cf], [r_cf], out=hbias[:, 0:20], in0=vecs[:, 50:70], scalar1=0.5, scalar2=None, op0=ALU.mult)

    r_rb, r_oh, r_ext, r_jm, r_sc = Res("rb"), Res("oh"), Res("ext"), Res("jm"), Res("sc")
    r_H = [Res("H0"), Res("H1")]
    DMA(rb[:], relb_d, writes=[r_rb], key="c_rb")
    DMA(oh[:], oh_d, writes=[r_oh], key="c_oh")
    DMA(jm[:], jmat_d, writes=[r_jm], key="c_jm")
    OP("pool", "memset", [], [r_ext], ap=ext[:], constant=NEG)
    for g in range(3):
        b = nextF()
        OP("pe", "matmul", [r_rb, r_oh], [bankF[b]], out=psF[0:12, b, 0:129], lhsT=rb[:], rhs=oh[:, g * 129:(g + 1) * 129], start=True, stop=True)
        OP("dve", "tensor_copy", [bankF[b]], [r_ext], out=ext[:, g, 127:256], in_=psF[0:12, b, 0:129])
    for g in range(3):
        DMA(sc_d[4 * g:4 * g + 4, :], ext[4 * g:4 * g + 4, g, :], reads=[r_ext], writes=[r_sc], key="c_sc")
    for gh in range(12):
        hb = gh % 2
        src = bass.AP(sc_d.tensor, gh * 383, [[1, 128], [1, 256]])
        DMA(Hb[hb][:], src, reads=[r_sc], writes=[r_H[hb]], key="c_H%d" % hb)
        b = nextF()
        OP("pe", "matmul", [r_jm, r_H[hb]], [bankF[b]], out=psF[:, b, 0:256], lhsT=jm[:], rhs=Hb[hb][:], start=True, stop=True)
        ACTV(Et[:, gh, :, :], psF[:, b, 0:256].rearrange("p (a b) -> p a b", b=128), AF.Exp, [bankF[b]], [r_E])
    if debug:
        r_dE = Res("dE")
        OP("dve", "tensor_copy", [r_E], [r_dE], out=dE[:], in_=Et[:].rearrange("p a b c -> p (a b c)"))
        dbg_toks.append(DMA(dbg["d_E"], dE[:], reads=[r_dE], key="dbg"))
        dbg_toks.append(DMA(dbg["d_cfac"][:, 0:16], cf[:, 0:16], reads=[r_cf], key="dbg"))

    out_toks = []
    base_dead = [r_tmpc, r_rb, r_oh, r_ext, r_jm, r_H[0], r_H[1]] + ([r_dE] if debug else [])
    prev_hn2T = None
    evi = [0]

    def evac_copy(dst, src, reads, writes, scale=None):
        evi[0] += 1
        if evi[0] % 2:
            if scale is not None:
                ACTV(dst, src, AF.Copy, reads, writes, scale=scale)
            else:
                ACTV(dst, src, AF.Copy, reads, writes)
        else:
            if scale is not None:
                OP("dve", "tensor_scalar", reads, writes, out=dst, in0=src, scalar1=scale, scalar2=None, op0=ALU.mult)
            else:
                OP("dve", "tensor_copy", reads, writes, out=dst, in_=src)

    def norm_stats(src_tile, r_src, hb, r_hb, r_st, scol):
        ss = stats[:, scol:scol + 1]
        sq = stats[:, scol + 1:scol + 2]
        rs = stats[:, scol + 2:scol + 3]
        ACTV(hb, src_tile, AF.Square, [r_src], [r_hb, r_st], accum_out=ss)
        ACTV(sq, ss, AF.Sqrt, [r_st], [r_st], scale=1.0 / D, bias=EPS)
        OP("dve", "reciprocal", [r_st], [r_st], out=rs, in_=sq)
        OP("dve", "tensor_scalar", [r_src, r_st], [r_hb], out=hb, in0=src_tile, scalar1=rs, scalar2=None, op0=ALU.mult)

    def transpose_to_T(hb, r_hb, gain, dstT, r_dst, tt):
        tb = nextT()
        for kc in range(8):
            OP("pe", "transpose", [r_hb, r_ident], [bankT[tb]], signal=(kc == 7),
               out=psT[:, tb, kc * 128:(kc + 1) * 128], in_=hb[:, kc * 128:(kc + 1) * 128], identity=ident_bf[:])
        OP("dve", "tensor_tensor", [bankT[tb], r_vecs], [r_dst], out=dstT[:, :, tt * 128:(tt + 1) * 128],
           in0=psT[:, tb, :].rearrange("p (a b) -> p a b", b=128), in1=bcast_free(gain, 128), op=ALU.mult)

    def proj_fm(wtile, nk, rhs_t, rhs_res, dst_fn, w_res, tgs=range(4)):
        for tg in tgs:
            b = nextF()
            for kc in range(nk):
                OP("pe", "matmul", [w_res] + rhs_res(tg), [bankF[b]], signal=(kc == nk - 1),
                   out=psF[:, b, :], lhsT=wtile[:, kc, :], rhs=rhs_t[:, kc, tg * 512:(tg + 1) * 512], start=(kc == 0), stop=(kc == nk - 1))
            dst_fn(tg, b)

    for s in range(nseq):
        us = useq[s]
        row0 = s * SEQ
        r_hnT = [Res("hnT%d" % t, after=base_dead + ([prev_hn2T[t]] if prev_hn2T else [])) for t in range(16)]
        r_yrnn = [Res("yrnn%d" % c, after=base_dead) for c in range(NRC)]
        r_oatt = [Res("oatt%d" % h, after=base_dead) for h in range(NH)]
        r_merged = [Res("merged%d" % m, after=base_dead) for m in range(8)]
        hn_res = lambda tg: r_hnT[tg * 4:(tg + 1) * 4]

        S.label = 'A'
        r_xt = [Res("xt%d" % i, after=base_dead) for i in range(2)]
        r_hnb = [Res("hnb%d" % i, after=base_dead) for i in range(2)]
        r_stA = [Res("statsA0"), Res("statsA1")]
        for tt in range(17):
            if tt < 16:
                bi = tt % 2
                DMA(xt[bi][:], x_d[row0 + tt * 128:row0 + (tt + 1) * 128, :], writes=[r_xt[bi]], key="xt%d" % bi)
                norm_stats(xt[bi][:], r_xt[bi], hnb[bi][:], r_hnb[bi], r_stA[bi], 4 * bi)
            if tt >= 1:
                t1 = tt - 1
                transpose_to_T(hnb[t1 % 2][:], r_hnb[t1 % 2], gpre, hnT, r_hnT[t1], t1)
        phaseA_res = r_xt + r_hnb
        if stop_after == "A":
            break

        dB = phaseA_res + base_dead
        r_q = [Res("q%d" % g, after=dB) for g in range(3)]
        r_k = Res("k", after=dB)
        r_v = Res("v", after=dB)
        r_Vb = [Res("Vb%d" % g, after=dB) for g in range(3)]
        r_acc = Res("acc", after=dB)
        r_Pf = [Res("Pf%d" % i, after=dB) for i in range(2)]
        r_Pb = [Res("Pb%d" % i, after=dB) for i in range(2)]
        r_rcp = Res("rcp", after=dB)
        NS_ = 3
        r_S = [bankF[3], bankF[4], bankF[5]]
        r_O = [bankF[6], bankF[7]]
        rot["modF"] = 3
        Sap = [psF[:, 3 + i, 0:256] for i in range(3)]
        Oap = [psF[:, 6 + i, 0:256] for i in range(2)]
        for h in range(NH):
            S.label = 'Bp'
            plan = [(0, 0, qT[0], r_q[0], SM_SCALE), (0, 1, qT[1], r_q[1], SM_SCALE), (1, 0, qT[2], r_q[2], SM_SCALE), (1, 1, kT, r_k, None), (2, 0, vT, r_v, None)]
            for (ui, sub, dstt, dres, scl) in plan:
                wt_, wres = WP.get(us["B"][h][ui])
                wt = wt_[:, sub * 1024:(sub + 1) * 1024].rearrange("p (a b) -> p a b", b=128)
                proj_fm(wt, 8, hnT, hn_res,
                        lambda tg, b, dstt=dstt, dres=dres, scl=scl: evac_copy(dstt[:, tg * 512:(tg + 1) * 512], psF[:, b, :], [bankF[b]], [dres], scale=scl),
                        wres)
            S.label = 'Bv'
            for g, (win, r) in enumerate(DIL):
                nb = SEQ // r // 128
                for half in range(2):
                    tb = nextT()
                    for jj in range(8):
                        blk = half * 8 + jj
                        c, n = blk // nb, blk % nb
                        t0 = c + r * 128 * n
                        OP("pe", "transpose", [r_v, r_ident], [bankT[tb]], signal=(jj == 7),
                           out=psT[:, tb, jj * 128:(jj + 1) * 128], in_=vT[:, t0:t0 + 127 * r + 1:r], identity=ident_bf[:])
                    evac_copy(Vb[g][:, half * 8:(half + 1) * 8, :], psT[:, tb, :].rearrange("p (a b) -> p a b", b=128), [bankT[tb]], [r_Vb[g]])
            S.label = 'Ba'
            blocks = []
            for g, (win, r) in enumerate(DIL):
                nb = SEQ // r // 128
                for blk in range(16):
                    c, n = blk // nb, blk % nb
                    blocks.append((g, r, blk, c + r * 128 * n, 2 if n > 0 else 1))
            NBLK = len(blocks)

            def st_qk(i):
                g, r, blk, t0, nkb = blocks[i]
                sp_ = Sap[i % NS_]
                qsl = qT[g][:, t0:t0 + 127 * r + 1:r]
                for kb in range(nkb):
                    tk = t0 - kb * r * 128
                    OP("pe", "matmul", [r_k, r_q[g]], [r_S[i % NS_]], signal=(kb == nkb - 1),
                       out=sp_[:, kb * 128:(kb + 1) * 128], lhsT=kT[:, tk:tk + 127 * r + 1:r], rhs=qsl, start=True, stop=True)

            def st_em(i):
                g, r, blk, t0, nkb = blocks[i]
                w = nkb * 128
                pi = i % 2
                Eflat = Et[:, g * 4 + h, :, :].rearrange("p a b -> p (a b)")
                ACTV(Pf[pi][:, 0:w], Sap[i % NS_][:, 0:w], AF.Exp, [r_S[i % NS_]], [r_Pf[pi]])
                OP("pool", "tensor_tensor", [r_Pf[pi], r_E], [r_Pb[pi]], out=Pb[pi][:, 0:w], in0=Pf[pi][:, 0:w], in1=Eflat[:, 0:w], op=ALU.mult)

            def st_pv(i):
                g, r, blk, t0, nkb = blocks[i]
                pi = i % 2
                op_ = Oap[i % 2]
                for part in range(2):
                    for kb in range(nkb):
                        lhs = Vb[g][:, blk - kb, :] if part == 0 else ones_bf[:]
                        OP("pe", "matmul", [r_Vb[g], r_ones, r_Pb[pi]], [r_O[i % 2]], signal=(part == 1 and kb == nkb - 1),
                           out=op_[:, part * 128:(part + 1) * 128], lhsT=lhs, rhs=Pb[pi][:, kb * 128:(kb + 1) * 128],
                           start=(kb == 0), stop=(kb == nkb - 1))
                osl = acc[:, :, t0:t0 + 127 * r + 1:r]
                psl = op_.rearrange("p (a b) -> p a b", b=128)
                if g == 0:
                    OP("dve", "tensor_copy", [r_O[i % 2]], [r_acc], out=osl, in_=psl)
                else:
                    OP("dve", "tensor_tensor", [r_O[i % 2], r_acc], [r_acc], out=osl, in0=osl, in1=psl, op=ALU.add)

            for step in range(NBLK + 2):
                if step < NBLK:
                    st_qk(step)
                if 0 <= step - 1 < NBLK:
                    st_em(step - 1)
                if 0 <= step - 2 < NBLK:
                    st_pv(step - 2)
            OP("dve", "reciprocal", [r_acc], [r_rcp], out=rcp[:], in_=acc[:, 1, :])
            OP("dve", "tensor_tensor", [r_acc, r_rcp], [r_oatt[h]], out=oatt[:, h, :], in0=acc[:, 0, :], in1=rcp[:], op=ALU.mult)
        rot["modF"] = 6
        phaseB_res = r_q + [r_k, r_v] + r_Vb + [r_acc] + r_Pf + r_Pb + [r_rcp]
        if stop_after == "B":
            break

        S.label = 'C'
        dC = phaseB_res + dB
        r_xr = [Res("xr%d" % i, after=dC) for i in range(NSET)]
        r_xc = [Res("xc%d" % i, after=dC) for i in range(NSET)]
        r_rr = [Res("rr%d" % i, after=dC) for i in range(NSET)]
        r_xi = [Res("xi%d" % i, after=dC) for i in range(NSET)]
        r_hcar = Res("hcar")
        for i in range(NSET):
            OP("pool", "memset", [], [r_xr[i]], ap=xr_h[i][:, 0:4], constant=0.0)
        NU = 2 * NRC
        cw_ = {}

        def c_s0(u):
            c, hf, k = u // 2, u % 2, u % NSET
            wt_, wres = WP.get(us["C"][c])
            cw_[u] = (wt_, wres)
            wt = wt_[:, 0:1024].rearrange("p (a b) -> p a b", b=128)
            if hf == 1:
                kp = (u - 1) % NSET
                OP("pool", "tensor_copy", [r_xr[kp]], [r_xr[k]], out=xr_h[k][:, 1:4], in_=xr_h[kp][:, HS + 1:HS + 4])
            else:
                OP("pool", "memset", [], [r_xr[k]], ap=xr_h[k][:, 0:4], constant=0.0)
            proj_fm(wt, 8, hnT, hn_res,
                    lambda tg, b: ACTV(xr_h[k][:, 4 + (tg % 2) * 512:4 + (tg % 2 + 1) * 512], psF[:, b, :], AF.Copy, [bankF[b]], [r_xr[k]]),
                    wres, tgs=(2 * hf, 2 * hf + 1))

        def c_s1(u):
            c, hf, k = u // 2, u % 2, u % NSET
            cw = [vecs[:, c * 4 + j:c * 4 + j + 1] for j in range(4)]
            OP("dve", "tensor_scalar", [r_xr[k], r_vecs], [r_xc[k]], out=xc_h[k][:], in0=xr_h[k][:, 4:4 + HS], scalar1=cw[3], scalar2=vecs[:, 40 + c:41 + c], op0=ALU.mult, op1=ALU.add)
            for j in range(3):
                OP("dve", "scalar_tensor_tensor", [r_xr[k], r_vecs, r_xc[k]], [r_xc[k]], out=xc_h[k][:], in0=xr_h[k][:, 1 + j:1 + j + HS], scalar=cw[j], in1=xc_h[k][:], op0=ALU.mult, op1=ALU.add)
            OP("dve", "tensor_copy", [r_xc[k]], [r_xi[k]], out=xi_h[k][:], in_=xc_h[k][:])

        def c_s2(u):
            c, hf, k = u // 2, u % 2, u % NSET
            wt_, wres = cw_[u]
            wa = wt_[:, 1024:1152]
            wx = wt_[:, 1152:1280]
            banks = []
            for t2 in range(2):
                ba, bx = nextF(), nextF()
                banks.append((ba, bx))
                sl = slice(t2 * 512, (t2 + 1) * 512)
                OP("pe", "matmul", [wres, r_xi[k]], [bankF[ba]], out=psF[:, ba, :], lhsT=wa, rhs=xi_h[k][:, sl], start=True, stop=True)
                OP("pe", "matmul", [wres, r_xi[k]], [bankF[bx]], out=psF[:, bx, :], lhsT=wx, rhs=xi_h[k][:, sl], start=True, stop=True)
            for t2 in range(2):
                ba, bx = banks[t2]
                sl = slice(t2 * 512, (t2 + 1) * 512)
                ACTV(rr_h[k][:, sl], psF[:, ba, :], AF.Tanh, [bankF[ba], r_cf], [r_rr[k]], scale=0.5, bias=hbias[:, c:c + 1])
                ACTV(xi_h[k][:, sl], psF[:, bx, :], AF.Tanh, [bankF[bx], r_cf], [r_xi[k]], scale=0.5, bias=hbias[:, 10 + c:11 + c])

        def c_s3(u):
            c, hf, k = u // 2, u % 2, u % NSET
            sq_ap = xr_h[k][:, 4:4 + HS]
            ACTV(sq_ap, rr_h[k][:], AF.Exp, [r_rr[k], r_cf], [r_xr[k]], scale=cf[:, c:c + 1], bias=cf[:, c:c + 1])
            ACTV(rr_h[k][:], rr_h[k][:], AF.Exp, [r_rr[k], r_cf], [r_rr[k]], scale=cf[:, 20 + c:21 + c], bias=cf[:, 20 + c:21 + c])
            ACTV(sq_ap, sq_ap, AF.Sqrt, [r_xr[k]], [r_xr[k]], scale=-0.25, bias=0.25)
            OP("dve", "scalar_tensor_tensor", [r_xi[k], r_xc[k]], [r_xc[k]], out=xc_h[k][:], in0=xi_h[k][:], scalar=1.0, in1=xc_h[k][:], op0=ALU.add, op1=ALU.mult)
            OP("dve", "tensor_tensor", [r_xc[k], r_xr[k]], [r_xc[k]], out=xc_h[k][:], in0=xc_h[k][:], in1=sq_ap, op=ALU.mult)

        def c_s4(u):
            c, hf, k = u // 2, u % 2, u % NSET
            y_ap = xr_h[k][:, 4:4 + HS]
            init = 0.0 if hf == 0 else hcar[:, 0:1]
            rds = [r_rr[k], r_xc[k]] + ([r_hcar] if hf == 1 else [])
            OP("dve", "tensor_tensor_scan", rds, [r_xr[k]], out=y_ap, data0=rr_h[k][:], data1=xc_h[k][:], initial=init, op0=ALU.mult, op1=ALU.add)
            if hf == 0:
                OP("pool", "tensor_copy", [r_xr[k]], [r_hcar], out=hcar[:, 0:1], in_=xr_h[k][:, 4 + HS - 1:4 + HS])
            OP("pool", "tensor_copy", [r_xr[k]], [r_yrnn[c]], out=yrnn[:, c, hf * HS:(hf + 1) * HS], in_=y_ap)

        for step in range(NU + 2):
            if step < NU:
                c_s0(step)
            if 0 <= step - 2 < NU:
                c_s4(step - 2)
            if 0 <= step - 1 < NU:
                c_s2(step - 1)
                c_s3(step - 1)
            if step < NU:
                c_s1(step)
        phaseC_res = r_xr + r_xc + r_rr + r_xi

        if stop_after == "C":
            break

        S.label = 'D1'
        dD = phaseC_res + dC
        r_tm = [[Res("tm%d_%d" % (a, i), after=dD) for i in range(2)] for a in range(2)]
        it = 0
        for m in range(8):
            wA, rA = WP.get(us["D1"][m][0])
            wB, rB = WP.get(us["D1"][m][1])
            wAv = wA[:, 0:2048].rearrange("p (a b) -> p a b", b=128)
            wBv = wB[:, 0:1792].rearrange("p (a b) -> p a b", b=128)
            for tg in range(4):
                sl = slice(tg * 512, (tg + 1) * 512)
                bgr, bga, bbr, bba = nextF(), nextF(), nextF(), nextF()
                for (b, wv, wr_, k0, nk, rhs_t, rres) in ((bgr, wAv, rA, 0, 8, hnT, hn_res(tg)), (bga, wAv, rA, 8, 8, hnT, hn_res(tg)),
                                                         (bbr, wBv, rB, 0, 10, yrnn, r_yrnn), (bba, wBv, rB, 10, 4, oatt, r_oatt)):
                    for kc in range(nk):
                        OP("pe", "matmul", [wr_] + list(rres), [bankF[b]], signal=(kc == nk - 1),
                           out=psF[:, b, :], lhsT=wv[:, k0 + kc, :], rhs=rhs_t[:, kc, sl], start=(kc == 0), stop=(kc == nk - 1))
                a = it % 2
                it += 1
                ACTV(tm[a][0][:], psF[:, bgr, :], AF.Sigmoid, [bankF[bgr]], [r_tm[a][0]])
                ACTV(tm[a][1][:], psF[:, bga, :], AF.Sigmoid, [bankF[bga]], [r_tm[a][1]])
                OP("dve", "tensor_tensor", [r_tm[a][0], bankF[bbr]], [r_tm[a][0]], out=tm[a][0][:], in0=tm[a][0][:], in1=psF[:, bbr, :], op=ALU.mult)
                OP("dve", "tensor_tensor", [r_tm[a][1], bankF[bba]], [r_tm[a][1]], out=tm[a][1][:], in0=tm[a][1][:], in1=psF[:, bba, :], op=ALU.mult)
                OP("pool", "tensor_tensor", [r_tm[a][0], r_tm[a][1]], [r_merged[m]], out=merged[:, m, sl], in0=tm[a][0][:], in1=tm[a][1][:], op=ALU.add)
        tm_res = r_tm[0] + r_tm[1]
        if debug and s == 0:
            r_dbuf = Res("dbuf", after=list(r_yrnn) + list(r_oatt))
            for cc in range(8):
                OP("dve", "tensor_copy", list(r_merged) + [r_dbuf], [r_dbuf], out=dbuf[:], in_=merged[:, cc, :])
                dbg_toks.append(DMA(dbg["d_merged"][:, cc * SEQ:(cc + 1) * SEQ], dbuf[:], reads=[r_dbuf], key="dbg"))
            dD = dD + [r_dbuf]
        if stop_after == "D1":
            break

        S.label = 'D2'
        dD2 = tm_res + dD + list(r_yrnn) + list(r_oatt)
        r_wdn = r_wdn_all[s]
        for rw_ in r_wdn:
            for o in dD2:
                if o.w is not None:
                    rw_._addr(o.w)
                for t_ in o.r.values():
                    rw_._addr(t_)
        for ui_ in us["Wd"]:
            WP.units[ui_]["hold"] = False
        r_xt2 = [Res("xt2_%d" % i, after=dD2) for i in range(2)]
        r_ht = [Res("ht%d" % i, after=dD2) for i in range(2)]
        r_hn2b = [Res("hn2b%d" % i, after=dD2) for i in range(2)]
        r_st2 = Res("st2")
        r_st2b = [Res("st2b0"), Res("st2b1")]
        r_hn2T = [Res("hn2T%d" % t, after=[r_hnT[t]]) for t in range(16)]
        r_hrow = [Res("hrow%d" % t) for t in range(16)]

        def d2_t0(tt):
            if tt < 12:
                WP.ensure(us["Wd"][tt])
            hb = tt % 2
            DMA(xt2[hb][:], x_d[row0 + tt * 128:row0 + (tt + 1) * 128, :], writes=[r_xt2[hb]], key="xt2_%d" % hb)
            p = nextP()
            for half in range(2):
                b = 2 * p + half
                for kc in range(8):
                    OP("pe", "matmul", list(r_merged) + [r_wout], [bankF[b]], signal=(kc == 7),
                       out=psF[:, b, :], lhsT=merged[:, kc, tt * 128:(tt + 1) * 128], rhs=w_out_bf[:, kc, half * 512:(half + 1) * 512], start=(kc == 0), stop=(kc == 7))
            mix = psF[:, 2 * p:2 * p + 2, :].rearrange("p a b -> p (a b)")
            pb = [bankF[2 * p], bankF[2 * p + 1]]
            ss, sq, rs = stats[:, 16:17], stats[:, 17:18], stats[:, 18:19]
            ACTV(hn2b[hb][:], mix, AF.Square, pb, [r_hn2b[hb], r_st2], accum_out=ss)
            ACTV(sq, ss, AF.Sqrt, [r_st2], [r_st2], scale=1.0 / D, bias=EPS)
            OP("dve", "reciprocal", [r_st2], [r_st2], out=rs, in_=sq)
            OP("dve", "scalar_tensor_tensor", pb + [r_st2, r_gpost], [r_ht[hb]], out=ht[hb][:], in0=mix, scalar=rs, in1=gpost[:], op0=ALU.mult, op1=ALU.mult)
            OP("pool", "tensor_tensor", [r_ht[hb], r_xt2[hb]], [r_ht[hb]], out=ht[hb][:], in0=ht[hb][:], in1=xt2[hb][:], op=ALU.add)
            DMA(out_d[row0 + tt * 128:row0 + (tt + 1) * 128, :], ht[hb][:], reads=[r_ht[hb]], writes=[r_hrow[tt]], key="hst%d" % hb)
            norm_stats(ht[hb][:], r_ht[hb], hn2b[hb][:], r_hn2b[hb], r_st2b[hb], 20 + 4 * hb)

        def d2_t1(tt):
            hb = tt % 2
            transpose_to_T(hn2b[hb][:], r_hn2b[hb], gpre2, hnT, r_hn2T[tt], tt)

        for step in range(17):
            if step < 16:
                d2_t0(step)
            if step >= 1:
                d2_t1(step - 1)
        phaseD2_res = r_xt2 + r_ht + r_hn2b
        if stop_after == "D2":
            out_toks = [r.w for r in r_hrow]
            break

        dE_ = phaseD2_res + dD2 + list(r_merged)
        r_gt = [Res("gt%d" % i, after=dE_) for i in range(2)]
        r_gc = [Res("gc%d" % i, after=dE_) for i in range(2)]
        r_ge = [Res("ge%d" % i, after=dE_) for i in range(2)]
        r_ot = Res("ot", after=dE_)
        r_hbuf = Res("hbuf", after=dE_)
        r_carry = Res("carry")
        r_st3 = Res("st3")
        r_ffT = [Res("ffT%d" % c, after=dE_) for c in range(NFC)]
        hn2_res = lambda tg: r_hn2T[tg * 4:(tg + 1) * 4]
        for tg in range(4):
            S.label = 'E'
            sl = slice(tg * 512, (tg + 1) * 512)
            for c in range(NFC):
                bi = c % 2
                wt_, wres = WP.get(us["E"][tg][c])
                wv = wt_[:, 0:2048].rearrange("p (a b) -> p a b", b=128)
                bg, bu = nextF(), nextF()
                for (b, k0) in ((bg, 0), (bu, 8)):
                    for kc in range(8):
                        OP("pe", "matmul", [wres] + hn2_res(tg), [bankF[b]], signal=(kc == 7),
                           out=psF[:, b, :], lhsT=wv[:, k0 + kc, :], rhs=hnT[:, kc, sl], start=(kc == 0), stop=(kc == 7))
                if tg == 0:
                    OP("pool", "memset", [], [r_gt[bi]], ap=gt[bi][:, 0:4], constant=0.0)
                else:
                    OP("pool", "tensor_copy", [r_carry], [r_gt[bi]], out=gt[bi][:, 2:4], in_=carry[:, c, :])
                ACTV(gt[bi][:, 4:516], psF[:, bg, :], AF.Copy, [bankF[bg]], [r_gt[bi]])
                if tg < 3:
                    OP("pool", "tensor_copy", [r_gt[bi]], [r_carry], out=carry[:, c, :], in_=gt[bi][:, 514:516])
                fw = [vecs[:, 80 + c * 3 + k:80 + c * 3 + k + 1] for k in range(3)]
                OP("dve", "tensor_scalar", [r_gt[bi], r_vecs], [r_gc[bi]], out=gc[bi][:], in0=gt[bi][:, 4:516], scalar1=fw[2], scalar2=vecs[:, 152 + c:153 + c], op0=ALU.mult, op1=ALU.add)
                for k in range(2):
                    OP("dve", "scalar_tensor_tensor", [r_gt[bi], r_vecs, r_gc[bi]], [r_gc[bi]], out=gc[bi][:], in0=gt[bi][:, 2 + k:2 + k + 512], scalar=fw[k], in1=gc[bi][:], op0=ALU.mult, op1=ALU.add)
                ACTV(ge[bi][:], gc[bi][:], AF.Gelu_apprx_tanh, [r_gc[bi]], [r_ge[bi]])
                OP("dve", "tensor_tensor", [r_ge[bi], bankF[bu]], [r_ffT[c]], out=ffT[:, c, :], in0=ge[bi][:], in1=psF[:, bu, :], op=ALU.mult)
            S.label = 'Ed'
            for t4 in range(4):
                tt = tg * 4 + t4
                DMA(hbuf[:], out_d[row0 + tt * 128:row0 + (tt + 1) * 128, :], reads=[r_hrow[tt]], writes=[r_hbuf], key="hbuf")
                p = nextP()
                for half in range(2):
                    b = 2 * p + half
                    for kc in range(NFC):
                        OP("pe", "matmul", list(r_ffT) + list(r_wdn), [bankF[b]], signal=(kc == NFC - 1),
                           out=psF[:, b, :], lhsT=ffT[:, kc, t4 * 128:(t4 + 1) * 128], rhs=wdn[:, kc, half * 512:(half + 1) * 512], start=(kc == 0), stop=(kc == NFC - 1))
                ff = psF[:, 2 * p:2 * p + 2, :].rearrange("p a b -> p (a b)")
                pb = [bankF[2 * p], bankF[2 * p + 1]]
                ss, sq, rs = stats[:, 32:33], stats[:, 33:34], stats[:, 34:35]
                ACTV(ot[:], ff, AF.Square, pb, [r_ot, r_st3], accum_out=ss)
                ACTV(sq, ss, AF.Sqrt, [r_st3], [r_st3], scale=1.0 / D, bias=EPS)
                OP("dve", "reciprocal", [r_st3], [r_st3], out=rs, in_=sq)
                OP("dve", "scalar_tensor_tensor", pb + [r_st3, r_gpost2], [r_ot], out=ot[:], in0=ff, scalar=rs, in1=gpost2[:], op0=ALU.mult, op1=ALU.mult)
                OP("pool", "tensor_tensor", [r_ot, r_hbuf], [r_ot], out=ot[:], in0=ot[:], in1=hbuf[:], op=ALU.add)
                out_toks.append(DMA(out_d[row0 + tt * 128:row0 + (tt + 1) * 128, :], ot[:], reads=[r_ot], writes=[r_hrow[tt]], key="ost"))
        base_dead = r_gt + r_gc + r_ge + [r_ot, r_hbuf] + list(r_ffT) + list(r_wdn)
        prev_hn2T = r_hn2T

    S.wait_tokens("sp", out_toks + dbg_toks)
    run_block(nc, S)
    SCHED_STATS.clear()
    SCHED_STATS.update({e: len(S.ops[e]) for e in ENGS})
    SCHED_STATS['labels'] = S.labels
    return nc


def host_consts(inp):
    f = lambda a: np.ascontiguousarray(np.asarray(a, dtype=np.float32))
    vecs = np.zeros((128, NV), np.float32)
    crw = f(inp["conv_rnn_w"])[0]
    vecs[:, 0:40] = crw.reshape(4, NRC, 128).transpose(2, 1, 0).reshape(128, 40)
    vecs[:, 40:50] = f(inp["conv_rnn_b"])[0].reshape(NRC, 128).T
    vecs[:, 50:60] = f(inp["b_rg_a"])[0].reshape(NRC, 128).T
    vecs[:, 60:70] = f(inp["b_rg_x"])[0].reshape(NRC, 128).T
    vecs[:, 70:80] = f(inp["lru_lambda"])[0].reshape(NRC, 128).T
    cfw = f(inp["conv_ffn_w"])[0]
    vecs[:, 80:152] = cfw.reshape(3, NFC, 128).transpose(2, 1, 0).reshape(128, 72)
    vecs[:, 152:176] = f(inp["conv_ffn_b"])[0].reshape(NFC, 128).T
    vecs[:, 176:184] = f(inp["norm_mix_pre"])[0].reshape(8, 128).T
    vecs[:, 184:192] = f(inp["norm_ffn_pre"])[0].reshape(8, 128).T
    oh = np.zeros((32, 3 * 129), np.float32)
    for g, (win, r) in enumerate(DIL):
        bk = _t5_bucket(np.arange(129) * r)
        oh[bk, g * 129 + np.arange(129)] = 1.0
    shared = {
        "w_in": f(inp["w_in"])[0],
        "w_rg_a": f(inp["w_rg_a"])[0].reshape(NRC * 128, 128),
        "w_rg_x": f(inp["w_rg_x"])[0].reshape(NRC * 128, 128),
        "w_branch_rnn": f(inp["w_branch_rnn"])[0],
        "w_branch_att": f(inp["w_branch_att"])[0],
        "w_out": f(inp["w_out"])[0],
        "w_ffn_gate": f(inp["w_ffn_gate"])[0],
        "w_ffn_up": f(inp["w_ffn_up"])[0],
        "w_ffn_down": f(inp["w_ffn_down"])[0],
        "rel_bias": f(inp["rel_bias"]),
        "oh": oh,
        "vecs": vecs,
        "gpost": np.ascontiguousarray(np.broadcast_to(f(inp["norm_mix_post"])[0], (128, D))),
        "gpost2": np.ascontiguousarray(np.broadcast_to(f(inp["norm_ffn_post"])[0], (128, D))),
        "ident": np.eye(128, dtype=np.float32),
        "jmat": np.ascontiguousarray(np.eye(128, dtype=np.float32)[::-1]),
    }
    return shared


_NC_CACHE = {}


def kernel(**inputs):
    x = np.asarray(inputs["x"], dtype=np.float32)
    B = x.shape[0]
    nseq = B // NCORES
    shared = host_consts(inputs)
    if nseq not in _NC_CACHE:
        _NC_CACHE[nseq] = build(nseq)
    nc = _NC_CACHE[nseq]
    in_maps = []
    for c in range(NCORES):
        m = dict(shared)
        m["x"] = np.ascontiguousarray(x[c * nseq:(c + 1) * nseq].reshape(nseq * SEQ, D))
        in_maps.append(m)
    res = run_bass_kernel_spmd(nc, in_maps, core_ids=list(range(NCORES)))
    outs = [np.asarray(r["out"]).reshape(nseq, SEQ, D) for r in res.results]
    return np.concatenate(outs, axis=0).astype(np.float32)
```
